# Optimizing a Trainium2 kernel written in Bass

```python
import jax, jax.numpy as jnp
from jax import lax
import numpy as np

D_MODEL = 1024
BATCH = 4
SEQ = 8192
DEPTH = 2

D_MIX = D_MODEL
D_FOURIER = D_MIX // 2
D_SSM = D_MIX - D_FOURIER
FOURIER_HEADS = 8
FOURIER_HEAD_DIM = D_FOURIER // FOURIER_HEADS
SSM_GROUP = 16
SSM_GROUPS = D_SSM // SSM_GROUP
SSM_STATE = 64
N_DIR = 2
D_FF = 4 * D_MODEL
DT_MIN = 1e-3
DT_MAX = 1e-1
EPS = 1e-6

kernel_name = "hybrid_fnet_s5_sandwich_encoder"


def rms_norm(x, g):
    x32 = x.astype(jnp.float32)
    y = x32 * lax.rsqrt(jnp.mean(x32 * x32, axis=-1, keepdims=True) + EPS)
    return (y * g.astype(jnp.float32)).astype(x.dtype)


def fourier_mixer(u, w_f):
    b, l, _ = u.shape
    uh = u.astype(jnp.float32).reshape(b, l, FOURIER_HEADS, FOURIER_HEAD_DIM)
    f = jnp.fft.fft(jnp.fft.fft(uh, axis=3, norm="ortho"), axis=1, norm="ortho").real
    y = jnp.einsum("blhc,hcd->blhd", f, w_f.astype(jnp.float32))
    return y.reshape(b, l, D_FOURIER).astype(u.dtype)


def _combine(left, right):
    a_l, b_l = left
    a_r, b_r = right
    return a_r * a_l, a_r * b_l + b_r


def s5_direction(u, lam_re, lam_im, log_dt, b_re, b_im, c_re, c_im, reverse):
    f32 = jnp.float32
    lam = lax.complex(lam_re.astype(f32), lam_im.astype(f32))
    dt = jnp.exp(log_dt.astype(f32))[:, None]
    lam_bar = jnp.exp(lam * dt)
    b_bar = ((lam_bar - 1.0) / lam)[..., None] * lax.complex(b_re.astype(f32), b_im.astype(f32))
    bu = jnp.einsum("blgh,gph->blgp", u, b_bar)
    a = jnp.broadcast_to(lam_bar, bu.shape)
    _, s = lax.associative_scan(_combine, (a, bu), axis=1, reverse=reverse)
    c = lax.complex(c_re.astype(f32), c_im.astype(f32))
    return jnp.einsum("blgp,ghp->blgh", s, c).real


def ssm_mixer(u, lam_re, lam_im, log_dt, b_re, b_im, c_re, c_im, d_skip, w_glu, b_glu):
    bsz, l, _ = u.shape
    u32 = u.astype(jnp.float32).reshape(bsz, l, SSM_GROUPS, SSM_GROUP)
    y = d_skip.astype(jnp.float32) * u32
    for k in range(N_DIR):
        y = y + s5_direction(u32, lam_re[k], lam_im[k], log_dt[k], b_re[k], b_im[k],
                             c_re[k], c_im[k], reverse=(k == 1))
    y = y.reshape(bsz, l, D_SSM)
    g = jax.nn.gelu(y)
    out = g * jax.nn.sigmoid(g @ w_glu.astype(jnp.float32) + b_glu.astype(jnp.float32))
    return out.astype(u.dtype)


def setup_inputs(seed: int = 0) -> dict:
    key = jax.random.key(seed)
    ks = jax.random.split(key, 24)
    f32 = jnp.float32
    nrm = lambda k, s, scale: (jax.random.normal(k, s, f32) * scale)
    gain = lambda k, s: 1.0 + 0.02 * jax.random.normal(k, s, f32)
    G, H, P = SSM_GROUPS, SSM_GROUP, SSM_STATE
    x = jax.random.normal(ks[0], (BATCH, SEQ, D_MODEL), f32)
    w_in = nrm(ks[1], (DEPTH, D_MODEL, D_MIX), D_MODEL ** -0.5)
    w_out = nrm(ks[2], (DEPTH, D_MIX, D_MODEL), D_MIX ** -0.5)
    pre_mix_g = gain(ks[3], (DEPTH, D_MODEL))
    post_mix_g = gain(ks[4], (DEPTH, D_MODEL))
    pre_mlp_g = gain(ks[5], (DEPTH, D_MODEL))
    post_mlp_g = gain(ks[6], (DEPTH, D_MODEL))
    fourier_out_g = gain(ks[7], (DEPTH, D_FOURIER))
    ssm_out_g = gain(ks[8], (DEPTH, D_SSM))
    w_fourier = nrm(ks[9], (DEPTH, FOURIER_HEADS, FOURIER_HEAD_DIM, FOURIER_HEAD_DIM),
                    FOURIER_HEAD_DIM ** -0.5)
    lam_re = -0.5 + 0.01 * jax.random.normal(ks[10], (DEPTH, N_DIR, G, P), f32)
    lam_im = (jnp.arange(P, dtype=f32) * np.pi)[None, None, None, :] + \
        0.01 * jax.random.normal(ks[11], (DEPTH, N_DIR, G, P), f32)
    log_dt = jax.random.uniform(ks[12], (DEPTH, N_DIR, G), f32,
                                minval=np.log(DT_MIN), maxval=np.log(DT_MAX))
    b_re = nrm(ks[13], (DEPTH, N_DIR, G, P, H), (2.0 * H) ** -0.5)
    b_im = nrm(ks[14], (DEPTH, N_DIR, G, P, H), (2.0 * H) ** -0.5)
    c_re = nrm(ks[15], (DEPTH, N_DIR, G, H, P), 0.5 ** 0.5)
    c_im = nrm(ks[16], (DEPTH, N_DIR, G, H, P), 0.5 ** 0.5)
    d_skip = nrm(ks[17], (DEPTH, G, H), 1.0)
    w_glu = nrm(ks[18], (DEPTH, D_SSM, D_SSM), D_SSM ** -0.5)
    b_glu = nrm(ks[19], (DEPTH, D_SSM), 0.01)
    w_ff1 = nrm(ks[20], (DEPTH, D_MODEL, D_FF), D_MODEL ** -0.5)
    w_ff2 = nrm(ks[21], (DEPTH, D_FF, D_MODEL), D_FF ** -0.5)
    return {"x": x, "w_in": w_in, "w_out": w_out, "pre_mix_g": pre_mix_g,
            "post_mix_g": post_mix_g, "pre_mlp_g": pre_mlp_g, "post_mlp_g": post_mlp_g,
            "fourier_out_g": fourier_out_g, "ssm_out_g": ssm_out_g, "w_fourier": w_fourier,
            "lam_re": lam_re, "lam_im": lam_im, "log_dt": log_dt, "b_re": b_re, "b_im": b_im,
            "c_re": c_re, "c_im": c_im, "d_skip": d_skip, "w_glu": w_glu, "b_glu": b_glu,
            "w_ff1": w_ff1, "w_ff2": w_ff2}


def reference(x, w_in, w_out, pre_mix_g, post_mix_g, pre_mlp_g, post_mlp_g,
              fourier_out_g, ssm_out_g, w_fourier, lam_re, lam_im, log_dt,
              b_re, b_im, c_re, c_im, d_skip, w_glu, b_glu, w_ff1, w_ff2):
    for i in range(DEPTH):
        h = rms_norm(x, pre_mix_g[i])
        z = h @ w_in[i]
        zf = z[..., :D_FOURIER]
        zs = z[..., D_FOURIER:]
        yf = rms_norm(fourier_mixer(zf, w_fourier[i]), fourier_out_g[i])
        ys = rms_norm(ssm_mixer(zs, lam_re[i], lam_im[i], log_dt[i], b_re[i], b_im[i],
                                c_re[i], c_im[i], d_skip[i], w_glu[i], b_glu[i]),
                      ssm_out_g[i])
        m = jnp.concatenate([yf, ys], axis=-1) @ w_out[i]
        x = x + rms_norm(m, post_mix_g[i])
        h = rms_norm(x, pre_mlp_g[i])
        f = jnp.square(jax.nn.relu(h @ w_ff1[i])) @ w_ff2[i]
        x = x + rms_norm(f, post_mlp_g[i])
    return x
```

```python
import math
import numpy as np
import ml_dtypes
import concourse.bass as bass
import concourse.mybir as mybir
from concourse.bass_utils import run_bass_kernel_spmd

F32 = mybir.dt.float32
BF16 = mybir.dt.bfloat16
AF = mybir.ActivationFunctionType
ALU = mybir.AluOpType

D = 1024
L = 8192
DEPTH = 2
DF = 512
DS = 512
NH = 8
G = 32
H = 16
P = 64
DFF = 4096
EPS = 1e-6
TT = 512
LO = L // 2
NT = LO // TT


class Buf:
    __slots__ = ("w", "r", "name")

    def __init__(self, name=""):
        self.w = None
        self.r = {}
        self.name = name


class Prog:
    ENG = ("pe", "act", "dve", "pool", "sp")

    def __init__(self, nc, n_dma_sems=48):
        self.nc = nc
        self.lists = {e: [] for e in self.ENG}
        self.cnt = {e: 0 for e in self.ENG}
        self.waited = {e: {} for e in self.ENG}
        self.n_dma = n_dma_sems
        self.dma_val = [0] * n_dma_sems
        self.dma_next = 0
        self.n_sw = 8
        self.n_cc = 8
        self.cc_next = 0
        self.sw_next = 0
        self.sems = {}
        self.keys = set()

    def _deps(self, eng, reads, writes):
        ev = {}

        def add(e):
            if e is None:
                return
            k, v = e
            if ev.get(k, 0) < v:
                ev[k] = v

        for b in reads:
            add(b.w)
        for b in writes:
            if not (eng == "pe" and b.w is not None and b.w[0][0] == "pe"):
                add(b.w)
            for k, v in b.r.items():
                if not (k[0] == "pe" and eng == "pe"):
                    add((k, v))
        wl = self.waited[eng]
        for k, v in ev.items():
            if wl.get(k, 0) < v:
                wl[k] = v
                self.lists[eng].append(("wait", k, v))

    def _commit(self, e, reads, writes):
        k, v = e
        for b in reads:
            if b.r.get(k, 0) < v:
                b.r[k] = v
        for b in writes:
            b.w = e
            b.r = {}

    EP = 2000

    def op(self, eng, fn, reads=(), writes=()):
        self._deps(eng, reads, writes)
        n = self.cnt[eng]
        self.cnt[eng] += 1
        key = (eng, n // self.EP)
        e = (key, n % self.EP + 1)
        self.keys.add(key)
        self.lists[eng].append(("op", fn, key, 1))
        self._commit(e, reads, writes)
        return e

    def dma(self, q, out, in_, reads=(), writes=(), custom=None, inc=16, **kw):
        if custom is not None:
            s = self.n_dma - self.n_sw - 1 - self.cc_next
            self.cc_next += 1
            assert self.cc_next <= self.n_cc
        elif q == "pool":
            s = self.n_dma - self.n_sw + self.sw_next
            self.sw_next = (self.sw_next + 1) % self.n_sw
        else:
            s = self.dma_next
            self.dma_next = (s + 1) % (self.n_dma - self.n_sw - self.n_cc)
        key = ("dma", s)
        if self.dma_val[s] > 0:
            wl = self.waited[q]
            if wl.get(key, 0) < self.dma_val[s]:
                wl[key] = self.dma_val[s]
                self.lists[q].append(("wait", key, self.dma_val[s]))
        self._deps(q, reads, writes)
        self.dma_val[s] += inc
        e = (key, self.dma_val[s])

        def fn(eng, out=out, in_=in_, kw=kw):
            if custom is not None:
                return custom(eng)
            return eng.dma_start(out=out, in_=in_, **kw)

        self.lists[q].append(("op", fn, key, inc))
        self._commit(e, reads, writes)
        return e

    def wait_event(self, eng, e):
        k, v = e
        wl = self.waited[eng]
        if wl.get(k, 0) < v:
            wl[k] = v
            self.lists[eng].append(("wait", k, v))

    def barrier(self, skip_cc=False):
        evs = []
        for e in self.ENG:
            n = self.cnt[e]
            if n > 0:
                ep = (n - 1) // self.EP
                evs.append(((e, ep), (n - 1) % self.EP + 1))
                if ep > 0:
                    evs.append(((e, ep - 1), self.EP))
        cc_slots = set(range(self.n_dma - self.n_sw - self.n_cc, self.n_dma - self.n_sw)) if skip_cc else set()
        evs += [(("dma", s), self.dma_val[s]) for s in range(self.n_dma) if self.dma_val[s] > 0 and s not in cc_slots]
        for eng in self.ENG:
            for e in evs:
                self.wait_event(eng, e)

    def emit(self, final_events=()):
        nc = self.nc
        import contextlib
        with contextlib.ExitStack() as st:
            sem = {}
            for key in sorted(self.keys):
                sem[key] = st.enter_context(nc.semaphore("s_%s_%d" % key))
            for s in range(self.n_dma):
                sem[("dma", s)] = st.enter_context(nc.semaphore("d%d" % s))
            block = st.enter_context(nc.Block())
            lists = self.lists

            def run(engobj, items):
                for it in items:
                    if it[0] == "wait":
                        engobj.wait_ge(sem[it[1]], it[2])
                    else:
                        ins = it[1](engobj)
                        ins.then_inc(sem[it[2]], it[3])

            @block.tensor
            def _(e):
                run(e, lists["pe"])

            @block.scalar
            def _(e):
                run(e, lists["act"])

            @block.vector
            def _(e):
                run(e, lists["dve"])

            @block.gpsimd
            def _(e):
                run(e, lists["pool"])

            @block.sync
            def _(e):
                run(e, lists["sp"])


def _bf(a):
    return np.ascontiguousarray(a, dtype=np.float32).astype(ml_dtypes.bfloat16)


def _consts(rank=0):
    c = {}
    c["ident_bf"] = _bf(np.eye(128))
    c["ident_f32"] = np.eye(128, dtype=np.float32)
    c["ones_bf"] = _bf(np.ones((128, 128)))
    l1 = np.arange(128)[:, None].astype(np.float64)
    k1 = np.arange(128)[None, :].astype(np.float64)
    ang = 2 * np.pi * l1 * k1 / 128.0
    s1 = 1.0 / np.sqrt(128.0)
    w1 = np.stack([np.cos(ang) * s1, -np.sin(ang) * s1], axis=-1)
    c["dft_w1"] = _bf(w1.reshape(128, 256))
    l2 = np.arange(64).astype(np.float64)
    k2 = np.arange(64).astype(np.float64)
    m2 = np.zeros((2, 64, 64, 2, 32, 2), dtype=np.float64)
    k2 = np.array([16 * c_ + 8 * rank + m_ for c_ in range(4) for m_ in range(8)], dtype=np.float64)
    s2 = 1.0 / np.sqrt(64.0)
    for b in range(2):
        k1v = (64 * b + np.arange(64)).astype(np.float64)
        kk = k1v[None, :, None] + 128.0 * k2[None, None, :]
        th = 2 * np.pi * kk * l2[:, None, None] / 8192.0
        mr = np.cos(th) * s2
        mi = -np.sin(th) * s2
        m2[b, :, :, 0, :, 0] = mr
        m2[b, :, :, 0, :, 1] = mi
        m2[b, :, :, 1, :, 0] = -mi
        m2[b, :, :, 1, :, 1] = mr
    c["dft_m2"] = _bf(m2.reshape(128, 64 * 2 * 64))
    msk = np.zeros((128, 2), dtype=np.float32)
    msk[:, rank] = 1.0
    c["s5_sel"] = msk
    cc = np.arange(64)[:, None].astype(np.float64)
    dd = np.arange(64)[None, :].astype(np.float64)
    a64 = 2 * np.pi * cc * dd / 64.0
    cs = np.zeros((64, 2, 128), dtype=np.float32)
    cs[:, 0, :64] = np.cos(a64) / 8.0
    cs[:, 0, 64:] = np.cos(a64) / 8.0
    cs[:, 1, :64] = np.sin(a64) / 8.0
    cs[:, 1, 64:] = np.sin(a64) / 8.0
    c["dft_cs"] = cs.reshape(64, 256)
    n = np.arange(40, dtype=np.float32)
    ev = np.zeros((2, 2, 40), dtype=np.float32)
    ev[0, 0] = 31.0 - n
    ev[0, 1] = n
    ev[1, 0] = n
    ev[1, 1] = 32.0 - n
    c["s5_ev"] = np.ascontiguousarray(np.broadcast_to(ev.reshape(1, 160), (128, 160)))
    tau = np.arange(128) // 16
    c["s5_maskf"] = (tau[None, :] >= tau[:, None]).astype(np.float32)
    c["s5_maskb"] = (tau[:, None] >= tau[None, :]).astype(np.float32)
    return c


class Ctx:
    pass


_uid = [0]


def _scope(C):
    import contextlib
    st = contextlib.ExitStack()
    nc = C.nc

    def sb(name, shape, dtype):
        _uid[0] += 1
        return st.enter_context(nc.sbuf_tensor("sb%d_%s" % (_uid[0], name), shape, dtype))

    def ps(name, shape, dtype):
        _uid[0] += 1
        return st.enter_context(nc.psum_tensor("ps%d_%s" % (_uid[0], name), shape, dtype))
    return st, sb, ps


def phase_A(C, layer, x_src, B_xsrc):
    P_ = C.P
    st, sb, ps = _scope(C)
    with st:
        ident_bf = C.ident_bf
        B_const = C.B_const
        g_pre = sb("A_g", [128, D], F32)
        B_g = Buf()
        P_.dma("sp", g_pre[:], C.ins["pre_mix_g"][layer].partition_broadcast(128), writes=[B_g])
        win = sb("A_win", [128, 8, D], BF16)
        B_win = Buf("win")
        winf = sb("A_winf", [128, 8, D], F32)
        B_winf = Buf()
        P_.dma("sp", winf[:], C.ins["w_in"][layer].rearrange("(kt p) c -> p kt c", p=128), writes=[B_winf])
        for kt in range(8):
            if kt % 2 == 0:
                P_.op("act", lambda e, kt=kt: e.copy(out=win[:, kt, :], in_=winf[:, kt, :]), reads=[B_winf], writes=[B_win])
            else:
                P_.op("dve", lambda e, kt=kt: e.tensor_copy(out=win[:, kt, :], in_=winf[:, kt, :]), reads=[B_winf], writes=[B_win])
        xt = [sb("A_xt%d" % i, [128, 4, D], F32) for i in range(2)]
        B_xt = [Buf() for _ in range(2)]
        junk = sb("A_junk", [128, D], BF16)
        B_junk = Buf()
        ss = [sb("A_ss%d" % i, [128, 4], F32) for i in range(2)]
        B_ss = [Buf() for _ in range(2)]
        rs = [sb("A_rs%d" % i, [128, 4], F32) for i in range(2)]
        B_rs = [Buf() for _ in range(2)]
        hb = [sb("A_hb%d" % i, [128, 4, D], BF16) for i in range(2)]
        B_hb = [Buf() for _ in range(2)]
        hT = [sb("A_hT%d" % i, [128, 8, TT], BF16) for i in range(2)]
        B_hT = [Buf() for _ in range(2)]
        zt = [sb("A_zt%d" % i, [128, 4, D], BF16) for i in range(2)]
        B_zt = [Buf() for _ in range(2)]
        pT = [ps("A_pT%d" % i, [128, 8, 128], BF16) for i in range(2)]
        B_pT = [Buf() for _ in range(2)]
        pz = [ps("A_pz%d" % i, [128, 512], F32) for i in range(4)]
        B_pz = [Buf() for _ in range(4)]

        B_Zc = [Buf() for _ in range(4)]

        def load(i):
            b = i % 2
            P_.dma("sp", xt[b][:], x_src[i * TT:(i + 1) * TT, :].rearrange("(s p) d -> p s d", p=128),
                   reads=[B_xsrc], writes=[B_xt[b]])
        load(0)
        cnt = {"npz": 0, "npt": 0}

        def front(i):
            b = i % 2
            if i + 1 < NT:
                load(i + 1)
            for s in range(4):
                P_.op("act", lambda e, b=b, s=s: e.activation(out=junk[:], in_=xt[b][:, s, :], func=AF.Square,
                                                         accum_out=ss[b][:, s:s + 1]),
                      reads=[B_xt[b]], writes=[B_junk, B_ss[b]])
            P_.op("dve", lambda e, b=b: e.tensor_scalar(out=rs[b][:], in0=ss[b][:], scalar1=1.0 / D, scalar2=EPS,
                                                        op0=ALU.mult, op1=ALU.add), reads=[B_ss[b]], writes=[B_rs[b]])
            P_.op("act", lambda e, b=b: e.activation(out=rs[b][:], in_=rs[b][:], func=AF.Sqrt), reads=[B_rs[b]], writes=[B_rs[b]])
            P_.op("dve", lambda e, b=b: e.reciprocal(out=rs[b][:], in_=rs[b][:]), reads=[B_rs[b]], writes=[B_rs[b]])
            for s in range(4):
                P_.op("dve", lambda e, b=b, s=s: e.scalar_tensor_tensor(
                    out=hb[b][:, s, :], in0=xt[b][:, s, :], scalar=rs[b][:, s:s + 1], in1=g_pre[:],
                    op0=ALU.mult, op1=ALU.mult), reads=[B_xt[b], B_rs[b], B_g], writes=[B_hb[b]])

        def back(i):
            b = i % 2
            for s in range(4):
                tb = cnt["npt"] % 2
                cnt["npt"] += 1
                for kt in range(8):
                    P_.op("pe", lambda e, b=b, s=s, kt=kt, tb=tb: e.transpose(
                        out=pT[tb][:, kt, :], in_=hb[b][:, s, kt * 128:(kt + 1) * 128], identity=ident_bf[:]),
                        reads=[B_hb[b], B_const], writes=[B_pT[tb]])
                if s % 2 == 0:
                    P_.op("act", lambda e, b=b, s=s, tb=tb: e.copy(out=hT[b][:, :, s * 128:(s + 1) * 128], in_=pT[tb][:]),
                          reads=[B_pT[tb]], writes=[B_hT[b]])
                else:
                    P_.op("dve", lambda e, b=b, s=s, tb=tb: e.tensor_copy(out=hT[b][:, :, s * 128:(s + 1) * 128], in_=pT[tb][:]),
                          reads=[B_pT[tb]], writes=[B_hT[b]])
            for s in range(4):
                for hf in range(2):
                    zb = cnt["npz"] % 4
                    cnt["npz"] += 1
                    for kt in range(8):
                        P_.op("pe", lambda e, b=b, s=s, hf=hf, kt=kt, zb=zb: e.matmul(
                            pz[zb][:], lhsT=hT[b][:, kt, s * 128:(s + 1) * 128], rhs=win[:, kt, hf * 512:(hf + 1) * 512],
                            start=(kt == 0), stop=(kt == 7)), reads=[B_hT[b], B_win], writes=[B_pz[zb]])
                    if hf == 0:
                        P_.op("act", lambda e, b=b, s=s, hf=hf, zb=zb: e.copy(out=zt[b][:, s, hf * 512:(hf + 1) * 512], in_=pz[zb][:]),
                              reads=[B_pz[zb]], writes=[B_zt[b]])
                    else:
                        P_.op("dve", lambda e, b=b, s=s, hf=hf, zb=zb: e.tensor_copy(out=zt[b][:, s, hf * 512:(hf + 1) * 512], in_=pz[zb][:]),
                              reads=[B_pz[zb]], writes=[B_zt[b]])
            P_.dma("sp", C.Zown[i * TT:(i + 1) * TT, :].rearrange("(s p) d -> p s d", p=128), zt[b][:],
                   reads=[B_zt[b]], writes=[C.B_Zown, B_Zc[i // 2]])
            if i % 2 == 1:
                c_ = i // 2
                P_.dma("pool", None, None, reads=[B_Zc[c_]], writes=[C.B_Z, C.B_Zg[c_]], inc=1,
                       custom=lambda e, c_=c_: e.collective_compute(
                           "AllGather", ALU.bypass, replica_groups=[[0, 1], [2, 3], [4, 5], [6, 7]],
                           ins=[C.Zown[c_ * 1024:(c_ + 1) * 1024, :]], outs=[C.Zs[c_ * 2048:(c_ + 1) * 2048, :]]))

        front(0)
        for i in range(NT):
            if i + 1 < NT:
                front(i + 1)
            back(i)
        P_.barrier(skip_cc=True)

def phase_BF(C, layer, after_loads=None):
    P_ = C.P
    st, sb, ps = _scope(C)
    with st:
        B_c = Buf()
        w1 = sb("F_w1", [128, 256], BF16)
        P_.dma("sp", w1[:], C.cins["dft_w1"].ap(), writes=[B_c])
        m2 = sb("F_m2", [128, 64, 2, 64], BF16)
        P_.dma("sp", m2[:], C.cins["dft_m2"].ap().rearrange("p (k r n) -> p k r n", k=64, r=2), writes=[B_c])
        cs = sb("F_cs", [64, 2, 128], F32)
        P_.dma("sp", cs[:], C.cins["dft_cs"].ap().rearrange("p (r n) -> p r n", r=2), writes=[B_c])
        wf = sb("F_wf", [64, NH, 64], F32)
        P_.dma("sp", wf[:], C.ins["w_fourier"][layer].rearrange("h d e -> d h e"), writes=[B_c])
        wf2 = sb("F_wf2", [128, 4, 2, 128], BF16)
        B_wf2 = Buf()
        P_.op("pool", lambda e: e.memset(wf2[:], 0.0), writes=[B_wf2])
        psw = ps("F_psw", [128, 16, 64], F32)
        B_psw = Buf()
        for h in range(NH):
            for ri in range(2):
                P_.op("pe", lambda e, h=h, ri=ri: e.matmul(psw[:, h * 2 + ri, :], lhsT=cs[:, ri, :], rhs=wf[:, h, :], start=True, stop=True),
                      reads=[B_c], writes=[B_psw])
        for h in range(NH):
            hp, a = h // 2, h % 2
            for ri in range(2):
                P_.op("dve", lambda e, h=h, ri=ri, hp=hp, a=a: e.tensor_copy(
                    out=wf2[64 * a:64 * a + 64, hp, ri, 64 * a:64 * a + 64], in_=psw[64 * a:64 * a + 64, h * 2 + ri, :]),
                    reads=[B_psw], writes=[B_wf2])

        zf = [sb("F_zf%d" % i, [128, 64, 128], BF16) for i in range(2)]
        B_zf = [Buf() for _ in range(2)]
        a2 = sb("F_a2", [128, 64, 2, 128], BF16)
        B_a2 = Buf()
        gt = sb("F_gt", [128, 2, LO], BF16)
        B_gt = Buf()
        ys = [sb("F_ys%d" % i, [128, 2048], F32) for i in range(2)]
        B_ys = [Buf() for _ in range(2)]
        p1 = [ps("F_p1%d" % i, [128, 4, 128], F32) for i in range(2)]
        B_p1 = [Buf() for _ in range(2)]
        p2 = [ps("F_p2%d" % i, [128, 4, 64], F32) for i in range(2)]
        B_p2 = [Buf() for _ in range(2)]
        p3 = [ps("F_p3%d" % i, [128, 512], F32) for i in range(2)]
        B_p3 = [Buf() for _ in range(2)]
        Zv = C.Zs.ap().rearrange("(l1 l2) c -> l1 l2 c", l2=64)

        def load(hp):
            P_.dma("sp", zf[hp % 2][:], Zv[:, :, hp * 128:(hp + 1) * 128], reads=C.B_Zg, writes=[B_zf[hp % 2]])
        load(0)
        if after_loads is not None:
            after_loads()
        n1 = n2 = n3 = 0
        for hp in range(4):
            zb = hp % 2
            if hp + 1 < 4:
                load(hp + 1)
            for c4 in range(32):
                pb = n1 % 2
                n1 += 1
                for cc in range(4):
                    c = c4 * 4 + cc
                    for b in range(2):
                        P_.op("pe", lambda e, zb=zb, c=c, cc=cc, b=b, pb=pb: e.matmul(
                            p1[pb][64 * b:64 * b + 64, cc, :], lhsT=zf[zb][:, :, c], rhs=w1[:, b * 128:(b + 1) * 128],
                            start=True, stop=True), reads=[B_zf[zb], B_c], writes=[B_p1[pb]])
                outap = a2[:, :, :, c4 * 4:(c4 + 1) * 4].rearrange("p k r c -> p c k r")
                inap = p1[pb][:].rearrange("p c (k r) -> p c k r", r=2)
                if c4 % 2 == 0:
                    P_.op("act", lambda e, outap=outap, inap=inap: e.copy(out=outap, in_=inap), reads=[B_p1[pb]], writes=[B_a2])
                else:
                    P_.op("dve", lambda e, outap=outap, inap=inap: e.tensor_copy(out=outap, in_=inap), reads=[B_p1[pb]], writes=[B_a2])
            gtv = gt[:].rearrange("c r (k2 k1) -> c k1 k2 r", k1=128)
            for q in range(32):
                pb = n2 % 2
                n2 += 1
                for kk in range(4):
                    k1 = q * 4 + kk
                    b, k1l = k1 // 64, k1 % 64
                    for ri in range(2):
                        P_.op("pe", lambda e, b=b, k1l=k1l, ri=ri, kk=kk, pb=pb: e.matmul(
                            p2[pb][:, kk, :], lhsT=a2[64 * b:64 * b + 64, k1l, ri, :], rhs=m2[64 * b:64 * b + 64, k1l, ri, :],
                            start=(ri == 0), stop=(ri == 1)), reads=[B_a2, B_c], writes=[B_p2[pb]])
                outap = gtv[:, q * 4:(q + 1) * 4, :, :]
                inap = p2[pb][:].rearrange("c k (k2 r) -> c k k2 r", r=2)
                if q % 2 == 0:
                    P_.op("act", lambda e, outap=outap, inap=inap: e.copy(out=outap, in_=inap), reads=[B_p2[pb]], writes=[B_gt])
                else:
                    P_.op("dve", lambda e, outap=outap, inap=inap: e.tensor_copy(out=outap, in_=inap), reads=[B_p2[pb]], writes=[B_gt])
            for tb in range(LO // 512):
                pb = n3 % 2
                n3 += 1
                yb = (hp * (LO // 2048) + tb // 4) % 2
                for ri in range(2):
                    P_.op("pe", lambda e, hp=hp, ri=ri, tb=tb, pb=pb: e.matmul(
                        p3[pb][:], lhsT=wf2[:, hp, ri, :], rhs=gt[:, ri, tb * 512:(tb + 1) * 512],
                        start=(ri == 0), stop=(ri == 1)), reads=[B_wf2, B_gt], writes=[B_p3[pb]])
                if tb % 2 == 0:
                    P_.op("act", lambda e, yb=yb, tb=tb, pb=pb: e.copy(out=ys[yb][:, (tb % 4) * 512:(tb % 4 + 1) * 512], in_=p3[pb][:]),
                          reads=[B_p3[pb]], writes=[B_ys[yb]])
                else:
                    P_.op("dve", lambda e, yb=yb, tb=tb, pb=pb: e.tensor_copy(out=ys[yb][:, (tb % 4) * 512:(tb % 4 + 1) * 512], in_=p3[pb][:]),
                          reads=[B_p3[pb]], writes=[B_ys[yb]])
                if tb % 4 == 3:
                    t0 = (tb // 4) * 2048
                    P_.dma("sp", C.YF[hp * 128:(hp + 1) * 128, t0:t0 + 2048], ys[yb][:], reads=[B_ys[yb]], writes=[C.B_YF])
        P_.barrier()


MAGIC = 12582912.0
TWO_PI = 2.0 * math.pi
CW1 = float(np.float32(6.28125))
CW2 = float(np.float32(TWO_PI - 6.28125))
CW3 = float(TWO_PI - CW1 - CW2)
JC = L // 32
POOL_TABLES = True
NOWLOAD = False
BS_PIPE = True


def phase_BS(C, layer):
    P_ = C.P
    nc = C.nc
    ins = C.ins
    ident_bf = C.ident_bf
    B_const = C.B_const
    st, sb, ps = _scope(C)

    def tt(eng, out, in0, in1, op, R, W):
        P_.op(eng, lambda e: e.tensor_tensor(out=out, in0=in0, in1=in1, op=op), reads=R, writes=W)

    def ts(eng, out, in0, s1, s2, op0, op1, R, W):
        if op1 is None:
            P_.op(eng, lambda e: e.tensor_scalar(out=out, in0=in0, scalar1=s1, scalar2=None, op0=op0), reads=R, writes=W)
        else:
            P_.op(eng, lambda e: e.tensor_scalar(out=out, in0=in0, scalar1=s1, scalar2=s2, op0=op0, op1=op1), reads=R, writes=W)

    def stt(out, in0, scalar, in1, op0, op1, R, W):
        P_.op("dve", lambda e: e.scalar_tensor_tensor(out=out, in0=in0, scalar=scalar, in1=in1, op0=op0, op1=op1), reads=R, writes=W)

    def actf(out, in_, func, R, W, **kw):
        P_.op("act", lambda e: e.activation(out=out, in_=in_, func=func, **kw), reads=R, writes=W)

    def acopy(out, in_, R, W):
        P_.op("act", lambda e: e.copy(out=out, in_=in_), reads=R, writes=W)

    with st:
        U_all = sb("S_U", [128, G, 1024], BF16)
        B_U = Buf("U")
        U_own = sb("S_Uo", [128, G, 512], BF16)
        B_Uo = Buf("Uo")
        ER = [sb("S_Er%d" % k, [128, 2, 16, 40], F32) for k in range(2)]
        EI = [sb("S_Ei%d" % k, [128, 2, 16, 40], F32) for k in range(2)]
        B_E = Buf("E")
        AKr = sb("S_AKr", [128, 8, 32], F32)
        AKi = sb("S_AKi", [128, 8, 32], F32)
        NAKi = sb("S_NAKi", [128, 8, 32], F32)
        B_AK = Buf("AK")
        bbr = sb("S_bbr", [128, 32, 16], F32)
        bbi = sb("S_bbi", [128, 32, 16], F32)
        B_bb = Buf("bb")
        Ctr = sb("S_Ctr", [128, 32, 16], F32)
        Cti = sb("S_Cti", [128, 32, 16], F32)
        B_Ct = Buf("Ct")
        Dcol = sb("S_Dcol", [128, 32], F32)
        B_D = Buf("D")
        maskf = sb("S_mf", [128, 128], F32)
        maskb = sb("S_mb", [128, 128], F32)
        identf = sb("S_idf", [128, 128], F32)
        B_mk = Buf("mk")
        P_.dma("sp", maskf[:], C.cins["s5_maskf"].ap(), writes=[B_mk])
        P_.dma("sp", maskb[:], C.cins["s5_maskb"].ap(), writes=[B_mk])
        P_.dma("sp", identf[:], C.cins["ident_f32"].ap(), writes=[B_mk])

        st0, sb0, ps0 = _scope(C)
        with st0:
            st0.close()
            st0b, sb0, ps0 = _scope(C)
            Bp = Buf("par")
            raw = sb0("S_raw", [32, 3, 128], F32)
            ldt = sb0("S_ldt", [32, 2], F32)
            ones32 = sb0("S_ones", [32, 64], F32)
            P_.op("pool", lambda e: e.memset(ones32[:], 1.0), writes=[Bp])
            P_.dma("sp", raw[:, 0, :], ins["lam_re"][layer].rearrange("d (gp a) p -> (d gp) (a p)", a=2), writes=[Bp])
            P_.dma("sp", raw[:, 1, :], ins["lam_im"][layer].rearrange("d (gp a) p -> (d gp) (a p)", a=2), writes=[Bp])
            P_.dma("sp", ldt[:], ins["log_dt"][layer].rearrange("d (gp a) -> (d gp) a", a=2), writes=[Bp])
            for a in range(2):
                ts("dve", raw[:, 2, a * 64:(a + 1) * 64], ones32[:], ldt[:, a:a + 1], None, ALU.mult, None, [Bp], [Bp])
            pp = ps0("S_pp", [128, 3, 32], F32)
            B_pp = Buf()
            for i in range(3):
                P_.op("pe", lambda e, i=i: e.transpose(out=pp[:, i, :], in_=raw[:, i, :], identity=identf[0:32, 0:32]),
                      reads=[Bp, B_mk], writes=[B_pp])
            lam = sb0("S_lam", [128, 3, 32], F32)
            P_.op("dve", lambda e: e.tensor_copy(out=lam[:], in_=pp[:]), reads=[B_pp], writes=[Bp])
            dtv = sb0("S_dtv", [128, 32], F32)
            actf(dtv[:], lam[:, 2, :], AF.Exp, [Bp], [Bp])
            alpha = sb0("S_alpha", [128, 32], F32)
            beta = sb0("S_beta", [128, 32], F32)
            tt("dve", alpha[:], lam[:, 0, :], dtv[:], ALU.mult, [Bp], [Bp])
            tt("dve", beta[:], lam[:, 1, :], dtv[:], ALU.mult, [Bp], [Bp])
            EV = sb0("S_EV", [128, 2, 2, 40], F32)
            P_.dma("sp", EV[:], C.cins["s5_ev"].ap().rearrange("p (k d n) -> p k d n", k=2, d=2), writes=[Bp])
            halfpi = sb0("S_hpi", [128, 1], F32)
            P_.op("pool", lambda e: e.memset(halfpi[:], math.pi / 2), writes=[Bp])
            arga = sb0("S_arga", [128, 2, 16, 40], F32)
            argb = sb0("S_argb", [128, 2, 16, 40], F32)
            kk = sb0("S_kk", [128, 2, 16, 40], F32)
            sn = sb0("S_sn", [128, 2, 16, 40], F32)
            shp = [128, 2, 16, 40]
            al4 = alpha[:].rearrange("q (d g) -> q d g", d=2).unsqueeze(3).broadcast_to(shp)
            be4 = beta[:].rearrange("q (d g) -> q d g", d=2).unsqueeze(3).broadcast_to(shp)
            for kind in range(2):
                ev4 = EV[:, kind].unsqueeze(2).broadcast_to(shp)
                tt("dve", arga[:], al4, ev4, ALU.mult, [Bp], [Bp])
                actf(arga[:], arga[:], AF.Exp, [Bp], [Bp])
                tt("dve", argb[:], be4, ev4, ALU.mult, [Bp], [Bp])
                ts("dve", kk[:], argb[:], 1.0 / TWO_PI, MAGIC, ALU.mult, ALU.add, [Bp], [Bp])
                ts("dve", kk[:], kk[:], -MAGIC, None, ALU.add, None, [Bp], [Bp])
                stt(argb[:], kk[:], -CW1, argb[:], ALU.mult, ALU.add, [Bp], [Bp])
                stt(argb[:], kk[:], -CW2, argb[:], ALU.mult, ALU.add, [Bp], [Bp])
                stt(argb[:], kk[:], -CW3, argb[:], ALU.mult, ALU.add, [Bp], [Bp])
                actf(sn[:], argb[:], AF.Sin, [Bp], [Bp])
                stt(kk[:], argb[:], -1.0, argb[:], ALU.mult, ALU.min, [Bp], [Bp])
                actf(kk[:], kk[:], AF.Sin, [Bp], [Bp], bias=halfpi[:], scale=1.0)
                tt("dve", ER[kind][:], arga[:], kk[:], ALU.mult, [Bp], [B_E])
                tt("dve", EI[kind][:], arga[:], sn[:], ALU.mult, [Bp], [B_E])
            ar = sb0("S_ar", [128, 32], F32)
            ai = sb0("S_ai", [128, 32], F32)
            for (dst, src) in ((ar, ER[1]), (ai, EI[1])):
                P_.op("dve", lambda e, dst=dst, src=src: e.tensor_copy(out=dst[:, 0:16], in_=src[:, 0, :, 1]), reads=[B_E], writes=[Bp])
                P_.op("dve", lambda e, dst=dst, src=src: e.tensor_copy(out=dst[:, 16:32], in_=src[:, 1, :, 31]), reads=[B_E], writes=[Bp])
            for (dst, src) in ((AKr, ER[1]), (AKi, EI[1])):
                P_.op("dve", lambda e, dst=dst, src=src: e.tensor_copy(out=dst[:, 0, 0:16], in_=src[:, 0, :, 32]), reads=[B_E], writes=[B_AK])
                P_.op("dve", lambda e, dst=dst, src=src: e.tensor_copy(out=dst[:, 0, 16:32], in_=src[:, 1, :, 0]), reads=[B_E], writes=[B_AK])
            t32a = sb0("S_t32a", [128, 32], F32)
            t32b = sb0("S_t32b", [128, 32], F32)
            for k in range(7):
                tt("dve", t32a[:], AKr[:, k, :], AKr[:, k, :], ALU.mult, [B_AK], [Bp])
                tt("dve", t32b[:], AKi[:, k, :], AKi[:, k, :], ALU.mult, [B_AK], [Bp])
                tt("dve", AKr[:, k + 1, :], t32a[:], t32b[:], ALU.subtract, [Bp], [B_AK])
                tt("dve", t32a[:], AKr[:, k, :], AKi[:, k, :], ALU.mult, [B_AK], [Bp])
                ts("dve", AKi[:, k + 1, :], t32a[:], 2.0, None, ALU.mult, None, [Bp], [B_AK])
            ts("dve", NAKi[:], AKi[:], -1.0, None, ALU.mult, None, [B_AK], [B_AK])
            wr = sb0("S_wr", [128, 32], F32)
            wi = sb0("S_wi", [128, 32], F32)
            den = sb0("S_den", [128, 32], F32)
            ts("dve", ar[:], ar[:], -1.0, None, ALU.add, None, [Bp], [Bp])
            tt("dve", den[:], lam[:, 0, :], lam[:, 0, :], ALU.mult, [Bp], [Bp])
            tt("dve", t32a[:], lam[:, 1, :], lam[:, 1, :], ALU.mult, [Bp], [Bp])
            tt("dve", den[:], den[:], t32a[:], ALU.add, [Bp], [Bp])
            P_.op("dve", lambda e: e.reciprocal(out=den[:], in_=den[:]), reads=[Bp], writes=[Bp])
            tt("dve", t32a[:], ar[:], lam[:, 0, :], ALU.mult, [Bp], [Bp])
            tt("dve", t32b[:], ai[:], lam[:, 1, :], ALU.mult, [Bp], [Bp])
            tt("dve", wr[:], t32a[:], t32b[:], ALU.add, [Bp], [Bp])
            tt("dve", wr[:], wr[:], den[:], ALU.mult, [Bp], [Bp])
            tt("dve", t32a[:], ai[:], lam[:, 0, :], ALU.mult, [Bp], [Bp])
            tt("dve", t32b[:], ar[:], lam[:, 1, :], ALU.mult, [Bp], [Bp])
            tt("dve", wi[:], t32a[:], t32b[:], ALU.subtract, [Bp], [Bp])
            tt("dve", wi[:], wi[:], den[:], ALU.mult, [Bp], [Bp])
            Btr = sb0("S_Btr", [128, 32, 16], F32)
            Bti = sb0("S_Bti", [128, 32, 16], F32)
            P_.dma("sp", Btr[:], ins["b_re"][layer].rearrange("d (gp a) p h -> (a p) (d gp) h", a=2), writes=[Bp])
            P_.dma("sp", Bti[:], ins["b_im"][layer].rearrange("d (gp a) p h -> (a p) (d gp) h", a=2), writes=[Bp])
            tb1 = sb0("S_tb1", [128, 32, 16], F32)
            tb2 = sb0("S_tb2", [128, 32, 16], F32)
            wr3 = wr[:].unsqueeze(2).broadcast_to([128, 32, 16])
            wi3 = wi[:].unsqueeze(2).broadcast_to([128, 32, 16])
            tt("dve", tb1[:], Btr[:], wr3, ALU.mult, [Bp], [Bp])
            tt("dve", tb2[:], Bti[:], wi3, ALU.mult, [Bp], [Bp])
            tt("dve", bbr[:], tb1[:], tb2[:], ALU.subtract, [Bp], [B_bb])
            tt("dve", tb1[:], Bti[:], wr3, ALU.mult, [Bp], [Bp])
            tt("dve", tb2[:], Btr[:], wi3, ALU.mult, [Bp], [Bp])
            tt("dve", bbi[:], tb1[:], tb2[:], ALU.add, [Bp], [B_bb])
            P_.barrier(skip_cc=True)
            st0b.close()
            st0c, sb0, ps0 = _scope(C)
            Craw = [sb0("S_Craw%d" % i, [16, 2 * G * P], F32) for i in range(2)]
            P_.dma("sp", Craw[0][:].rearrange("h (d g p) -> h d g p", d=2, g=G), ins["c_re"][layer].rearrange("d g h p -> h d g p"), writes=[Bp])
            P_.dma("sp", Craw[1][:].rearrange("h (d g p) -> h d g p", d=2, g=G), ins["c_im"][layer].rearrange("d g h p -> h d g p"), writes=[Bp])
            pc = [ps0("S_pc%d" % i, [128, 32, 16], F32) for i in range(2)]
            B_pc = [Buf() for _ in range(2)]
            for i in range(2):
                for dg in range(32):
                    P_.op("pe", lambda e, i=i, dg=dg: e.transpose(out=pc[i][:, dg, :], in_=Craw[i][:, dg * 128:(dg + 1) * 128],
                                                                  identity=identf[0:16, 0:16]), reads=[Bp, B_mk], writes=[B_pc[i]])
            P_.op("dve", lambda e: e.tensor_copy(out=Ctr[:], in_=pc[0][:]), reads=[B_pc[0]], writes=[B_Ct])
            P_.op("dve", lambda e: e.tensor_copy(out=Cti[:], in_=pc[1][:]), reads=[B_pc[1]], writes=[B_Ct])
            Dsm = sb0("S_Dsm", [32, 16], F32)
            Drep = sb0("S_Drep", [32, 8, 16], F32)
            P_.dma("sp", Dsm[:], ins["d_skip"][layer], writes=[Bp])
            P_.op("dve", lambda e: e.tensor_copy(out=Drep[:], in_=Dsm[:].unsqueeze(1).broadcast_to([32, 8, 16])), reads=[Bp], writes=[Bp])
            pd = ps0("S_pd", [128, 32], F32)
            B_pd = Buf()
            P_.op("pe", lambda e: e.transpose(out=pd[:], in_=Drep[:].rearrange("g t h -> g (t h)"), identity=identf[0:32, 0:32]),
                  reads=[Bp, B_mk], writes=[B_pd])
            P_.op("dve", lambda e: e.tensor_copy(out=Dcol[:], in_=pd[:]), reads=[B_pd], writes=[B_D])
            P_.barrier(skip_cc=True)
            st0c.close()
            st0d, sb0, ps0 = _scope(C)
            Zt = [sb0("S_Zt%d" % i, [128, 8, 512], BF16) for i in range(2)]
            B_Zt = [Buf() for _ in range(2)]
            Zt2 = [sb0("S_Zt2%d" % i, [128, G * 128], BF16) for i in range(2)]
            B_Zt2 = [Buf() for _ in range(2)]
            pT = [ps0("S_pT%d" % i, [128, 8, 128], BF16) for i in range(2)]
            B_pT = [Buf() for _ in range(2)]
            npt = 0
            for (Zsrc, Bsrc, Udst, Bdst, nblk) in ((C.Zown, C.B_Zown, U_own, B_Uo, 4), (C.Zs, C.B_Zg, U_all, B_U, 8)):
                Zv = Zsrc.ap().rearrange("(j t) c -> j t c", t=8)
                for bj in range(nblk):
                    b = npt % 2
                    P_.dma("sp", Zt[b][:], Zv[bj * 128:(bj + 1) * 128, :, 512:1024], reads=[Bsrc[bj // 2]] if isinstance(Bsrc, list) else [Bsrc], writes=[B_Zt[b]])
                    P_.op("act", lambda e, b=b: e.copy(
                        out=Zt2[b][:].rearrange("j (g t h) -> j g t h", g=G, t=8),
                        in_=Zt[b][:].rearrange("j t (g h) -> j g t h", h=16)), reads=[B_Zt[b]], writes=[B_Zt2[b]])
                    for g0 in range(0, G, 8):
                        tb = npt % 2
                        npt += 1
                        for gi in range(8):
                            g = g0 + gi
                            P_.op("pe", lambda e, b=b, g=g, gi=gi, tb=tb: e.transpose(
                                out=pT[tb][:, gi, :], in_=Zt2[b][:, g * 128:(g + 1) * 128], identity=ident_bf[:]),
                                reads=[B_Zt2[b], B_const], writes=[B_pT[tb]])
                        dst = Udst[:, g0:g0 + 8, bj * 128:(bj + 1) * 128]
                        if (g0 // 8) % 2 == 0:
                            acopy(dst, pT[tb][:], [B_pT[tb]], [Bdst])
                        else:
                            P_.op("dve", lambda e, dst=dst, tb=tb: e.tensor_copy(out=dst, in_=pT[tb][:]), reads=[B_pT[tb]], writes=[Bdst])

            P_.barrier()
            st0d.close()
            if C.dbg is not None:
                o = 0
                for (nm, t_, n_) in (("ERB", ER[0], 1280), ("EIB", EI[0], 1280), ("ERC", ER[1], 1280), ("EIC", EI[1], 1280),
                                     ("bbr", bbr, 512), ("bbi", bbi, 512), ("Ctr", Ctr, 512), ("Cti", Cti, 512),
                                     ("Dcol", Dcol, 32), ("AKr", AKr, 256), ("AKi", AKi, 256)):
                    flat = t_[:]
                    if len(flat.shape) == 4:
                        flat = flat.rearrange("p a b c -> p (a b c)")
                    elif len(flat.shape) == 3:
                        flat = flat.rearrange("p a b -> p (a b)")
                    P_.dma("sp", C.dbg[:, o:o + n_], flat, reads=[B_E, B_bb, B_Ct, B_D, B_AK], writes=[Buf()])
                    C.dbg_map[nm] = (o, n_)
                    o += n_
                for g_ in range(4):
                    P_.dma("sp", C.dbgU[:, g_, :], U_all[:, g_ * 9, :], reads=[B_U], writes=[Buf()])
                P_.barrier()
            if C.bs_level == 0:
                return

        Yall = sb("S_Y", [128, 32, 64], BF16)
        se_own = [sb("S_seo%d" % i, [128, 2, 2, 128], BF16) for i in range(2)]
        B_seo = [Buf() for _ in range(2)]
        setmp = sb("S_setmp", [128, 4, 4, 32], F32)
        B_setmp = Buf()
        sel = sb("S_sel", [128, 2], F32)
        P_.dma("sp", sel[:], C.cins["s5_sel"].ap(), writes=[B_mk])

        B_Y = Buf("Yall")
        t1 = sb("S_t1", [128, 2, 40, 16], F32)
        t2 = sb("S_t2", [128, 2, 40, 16], F32)
        B_t = [Buf(), Buf()]
        t3 = sb("S_t3", [128, 2, 40, 16], F32)
        t4 = sb("S_t4", [128, 2, 40, 16], F32)
        B_t34 = [Buf(), Buf()]
        TAB = [[sb("S_tab%d_%d" % (s_, i), [128, 2, 640], BF16) for i in range(4)] for s_ in range(2)]
        B_TAB = [[Buf() for _ in range(4)] for _ in range(2)]
        Mpan = [[sb("S_M%d_%d" % (s_, a), [128, 7 * 128], BF16) for a in range(2)] for s_ in range(2)]
        B_M = [[Buf() for _ in range(2)] for _ in range(2)]
        Bm = [[sb("S_Bm%d_%d" % (s_, a), [128, 16, 64], BF16) for a in range(2)] for s_ in range(2)]
        B_Bm = [[Buf() for _ in range(2)] for _ in range(2)]
        SA = sb("S_SA", [128, 2, 2, JC], F32)
        SB_ = sb("S_SB", [128, 2, 2, JC], F32)
        B_SA = Buf()
        B_SB = Buf()
        Sent = [sb("S_Sent%d" % i, [128, 2, 2, JC], BF16) for i in range(2)]
        B_Sent = [Buf() for _ in range(2)]
        dtmp = sb("S_dtmp", [128, 128], F32)
        dtmp2 = sb("S_dtmp2", [128, 128], F32)
        B_dt = Buf()
        pm = ps("S_pm", [128, 8, 128], F32)
        B_pm = Buf()
        pbm = ps("S_pbm", [128, 16, 64], BF16)
        B_pbm = Buf()
        Xps = ps("S_X", [128, 2, 2, JC], F32)
        B_X = Buf()
        Yps = [ps("S_Yp%d" % i, [128, 512], F32) for i in range(2)]
        B_Yp = [Buf() for _ in range(2)]
        for i in range(2):
            P_.op("pool", lambda e, i=i: e.memset(Sent[i][:], 0.0), writes=[B_Sent[i]])
        Uv = U_all[:].rearrange("p g (j q) -> p g q j", q=4)
        bbr4 = bbr[:].rearrange("q (d g) h -> q d g h", d=2)
        bbi4 = bbi[:].rearrange("q (d g) h -> q d g h", d=2)
        Ctr4 = Ctr[:].rearrange("q (d g) h -> q d g h", d=2)
        Cti4 = Cti[:].rearrange("q (d g) h -> q d g h", d=2)
        shp = [128, 2, 40, 16]
        nyp = 0
        YSv = C.YS.ap().rearrange("(j t) c -> j t c", t=32)
        specs = (
            (0, 0, bbr4, bbi4, "sub"),
            (1, 0, bbi4, bbr4, "add"),
            (2, 1, Ctr4, Cti4, "sub"),
            (3, 1, Cti4, Ctr4, "nadd"),
        )
        slots = []
        for m in (3, 2, 1):
            slots.append((1, 0, 1, 32 - 8 * m))
        slots.append((0, 31, 0, 0))
        slots.append((1, 0, 1, 32))
        for dl in (1, 2, 3):
            slots.append((0, 24, 0, 8 * dl - 7))

        def stage_T(gp):
            s_ = gp % 2
            tab, btab = TAB[s_], B_TAB[s_]
            for (oi, kind, Y1, Y2, comb) in specs:
                e1 = ER[kind][:, :, gp, :].unsqueeze(3).broadcast_to(shp)
                p1 = Y1[:, :, gp, :].unsqueeze(2).broadcast_to(shp)
                e2 = EI[kind][:, :, gp, :].unsqueeze(3).broadcast_to(shp)
                p2 = Y2[:, :, gp, :].unsqueeze(2).broadcast_to(shp)
                if oi < 2 and POOL_TABLES:
                    ta, tb_, Bt_, peng = t3, t4, B_t34, "pool"
                else:
                    ta, tb_, Bt_, peng = t1, t2, B_t, "dve"
                tt(peng, ta[:], e1, p1, ALU.mult, [B_E, B_bb, B_Ct], [Bt_[0]])
                tt(peng, tb_[:], e2, p2, ALU.mult, [B_E, B_bb, B_Ct], [Bt_[1]])
                outv = tab[oi][:].rearrange("q d (n h) -> q d n h", h=16)
                if comb == "sub":
                    tt(peng, outv, ta[:], tb_[:], ALU.subtract, Bt_, [btab[oi]])
                elif comb == "add":
                    tt(peng, outv, ta[:], tb_[:], ALU.add, Bt_, [btab[oi]])
                else:
                    stt(outv, ta[:], -1.0, tb_[:], ALU.mult, ALU.subtract, Bt_, [btab[oi]])

        def stage_PX(gp):
            s_ = gp % 2
            tab, btab = TAB[s_], B_TAB[s_]
            for gpar in range(2):
                g = 2 * gp + gpar
                for d in range(2):
                    for q in range(4):
                        for x in range(2):
                            idx = (d * 4 + q) * 2 + x
                            P_.op("pe", lambda e, idx=idx, gpar=gpar, i_=tab[x][64 * gpar:64 * gpar + 64, d, q * 128:(q + 1) * 128]: e.transpose(
                                out=pbm[:, idx, :], in_=i_,
                                identity=ident_bf[64 * gpar:64 * gpar + 64, 64 * gpar:64 * gpar + 64]),
                                reads=[btab[0], btab[1], B_const], writes=[B_pbm])
                acopy(Bm[s_][gpar][:], pbm[:], [B_pbm], [B_Bm[s_][gpar]])
                for d in range(2):
                    for x in range(2):
                        for q in range(4):
                            idx = (d * 4 + q) * 2 + x
                            P_.op("pe", lambda e, d=d, x=x, q=q, gpar=gpar, l_=Bm[s_][gpar][:, idx, :], r_=Uv[:, g, q, :]: e.matmul(
                                Xps[64 * gpar:64 * gpar + 64, d, x, :], lhsT=l_, rhs=r_,
                                start=(q == 0), stop=(q == 3)), reads=[B_Bm[s_][gpar], B_U], writes=[B_X])

        def stage_PM(gp):
            s_ = gp % 2
            tab, btab = TAB[s_], B_TAB[s_]

            def bsl(oi, gpar, d, n0):
                return tab[oi][64 * gpar:64 * gpar + 64, d, n0 * 16:(n0 + 8) * 16]
            for gpar in range(2):
                g = 2 * gp + gpar
                for si, (db, nb, dc, ncc) in enumerate(slots):
                    for x in range(2):
                        P_.op("pe", lambda e, si=si, x=x, l_=bsl(0 + x, gpar, db, nb), r_=bsl(2 + x, gpar, dc, ncc): e.matmul(
                            pm[:, si, :], lhsT=l_, rhs=r_,
                            start=(x == 0), stop=(x == 1)), reads=[btab[0], btab[1], btab[2], btab[3]], writes=[B_pm])
                Mp3 = Mpan[s_][gpar][:].rearrange("p (s n) -> p s n", s=7)
                acopy(Mp3[:, 0:3, :], pm[:, 0:3, :], [B_pm], [B_M[s_][gpar]])
                acopy(Mp3[:, 4:7, :], pm[:, 5:8, :], [B_pm], [B_M[s_][gpar]])
                acopy(dtmp[:], pm[:, 3, :], [B_pm], [B_dt])
                acopy(dtmp2[:], pm[:, 4, :], [B_pm], [B_dt])
                tt("dve", dtmp[:], dtmp[:], maskf[:], ALU.mult, [B_dt, B_mk], [B_dt])
                tt("dve", dtmp2[:], dtmp2[:], maskb[:], ALU.mult, [B_dt, B_mk], [B_dt])
                tt("dve", dtmp[:], dtmp[:], dtmp2[:], ALU.add, [B_dt], [B_dt])
                stt(Mp3[:, 3, :], identf[:], Dcol[:, g:g + 1], dtmp[:], ALU.mult, ALU.add, [B_dt, B_D, B_mk], [B_M[s_][gpar]])

        def stage_SC(gp):
            s_ = gp % 2
            acopy(SA[:], Xps[:], [B_X], [B_SA])
            Bm_ = {(bn, d, x): Buf() for bn in range(2) for d in range(2) for x in range(2)}
            Bu_ = {(bn, d): Buf() for bn in range(2) for d in range(2)}
            for key in Bm_:
                if key[0] == 0:
                    Bm_[key].w = B_SA.w
            bufs = (SA, SB_)
            for k in range(8):
                sh = 1 << k
                si, di = k % 2, (k + 1) % 2
                src, dst = bufs[si], bufs[di]
                plan = []
                for d in range(2):
                    col = d * 16 + gp
                    if d == 0:
                        lo, shf, unt = slice(sh, JC), slice(0, JC - sh), slice(0, sh)
                    else:
                        lo, shf, unt = slice(0, JC - sh), slice(sh, JC), slice(JC - sh, JC)
                    plan.append((d, lo, shf, unt, AKr[:, k, col:col + 1], AKi[:, k, col:col + 1], NAKi[:, k, col:col + 1]))

                def rd(d, si=si):
                    return [Bm_[(si, d, 0)], Bm_[(si, d, 1)], Bu_[(si, d)], B_AK]
                for (d, lo, shf, unt, cr, ci, nci) in plan:
                    stt(dst[:, d, 0, lo], src[:, d, 0, shf], cr, src[:, d, 0, lo], ALU.mult, ALU.add, rd(d), [Bm_[(di, d, 0)]])
                    stt(dst[:, d, 1, lo], src[:, d, 1, shf], cr, src[:, d, 1, lo], ALU.mult, ALU.add, rd(d), [Bm_[(di, d, 1)]])
                for (d, lo, shf, unt, cr, ci, nci) in plan:
                    stt(dst[:, d, 0, lo], src[:, d, 1, shf], nci, dst[:, d, 0, lo], ALU.mult, ALU.add, rd(d), [Bm_[(di, d, 0)]])
                    stt(dst[:, d, 1, lo], src[:, d, 0, shf], ci, dst[:, d, 1, lo], ALU.mult, ALU.add, rd(d), [Bm_[(di, d, 1)]])
                    acopy(dst[:, d, :, unt], src[:, d, :, unt], rd(d), [Bu_[(di, d)]])
            parts = [Bm_[(0, d, x)] for d in range(2) for x in range(2)] + [Bu_[(0, d)] for d in range(2)]
            se = Sent[s_]
            acopy(se[:, 0, :, 1:JC], SA[:, 0, :, 0:JC - 1], parts, [B_Sent[s_]])
            acopy(se[:, 1, :, 0:JC - 1], SA[:, 1, :, 1:JC], parts, [B_Sent[s_]])
            B_SA.w = None
            B_SA.r = {}
            for b__ in list(Bm_.values()) + list(Bu_.values()):
                for e__ in ([b__.w] if b__.w else []) + list(b__.r.items()):
                    if B_SA.r.get(e__[0], 0) < e__[1]:
                        B_SA.r[e__[0]] = e__[1]
            sev = se[:].rearrange("q d x (c r j) -> q (d x) c r j", c=4, r=2)
            seo = se_own[s_]
            ts("dve", setmp[:], sev[:, :, :, 0, :], sel[:, 0:1], None, ALU.mult, None, [B_Sent[s_], B_mk], [B_setmp])
            stt(seo[:].rearrange("q d x (c j) -> q (d x) c j", c=4), sev[:, :, :, 1, :], sel[:, 1:2], setmp[:], ALU.mult, ALU.add,
                [B_Sent[s_], B_mk, B_setmp], [B_seo[s_]])

        nyp_ = [0]

        def stage_OUT(gp):
            s_ = gp % 2
            tab, btab = TAB[s_], B_TAB[s_]
            seo = se_own[s_]
            for gpar in range(2):
                g = 2 * gp + gpar
                Mp = Mpan[s_][gpar]
                yb = nyp_[0] % 2
                nyp_[0] += 1
                for k in range(4):
                    P_.op("pe", lambda e, k=k, yb=yb, l_=U_own[:, g, k:512:4], r_=Mp[:, (3 - k) * 128:(7 - k) * 128]: e.matmul(
                        Yps[yb][:], lhsT=l_, rhs=r_, start=(k == 0), stop=False), reads=[B_Uo, B_M[s_][gpar]], writes=[B_Yp[yb]])
                for d in range(2):
                    n0 = 1 if d == 0 else 0
                    for x in range(2):
                        P_.op("pe", lambda e, d=d, x=x, yb=yb, l_=seo[64 * gpar:64 * gpar + 64, d, x, :],
                              r_=tab[2 + x][64 * gpar:64 * gpar + 64, d, n0 * 16:n0 * 16 + 512]: e.matmul(
                            Yps[yb][:], lhsT=l_, rhs=r_,
                            start=False, stop=(d == 1 and x == 1)), reads=[B_seo[s_], btab[2], btab[3]], writes=[B_Yp[yb]])
                gl = g % 4
                acopy(Yall[:, :, gl * 16:(gl + 1) * 16], Yps[yb][:].rearrange("j (t h) -> j t h", h=16), [B_Yp[yb]], [B_Y])
            if gp % 2 == 1:
                q8 = gp // 2
                P_.dma("sp", YSv[:, :, q8 * 64:(q8 + 1) * 64], Yall[:], reads=[B_Y], writes=[C.B_YS])

        if not BS_PIPE:
            for gp in range(16):
                stage_T(gp)
                stage_PM(gp)
                stage_PX(gp)
                stage_SC(gp)
                stage_OUT(gp)
        else:
            stage_T(0)
            stage_PX(0)
            stage_PM(0)
            for gp in range(16):
                if gp + 1 < 16:
                    stage_T(gp + 1)
                stage_SC(gp)
                if gp + 1 < 16:
                    stage_PX(gp + 1)
                    stage_PM(gp + 1)
                stage_OUT(gp)
        P_.barrier()


def prefetch_C_consts(C, layer, sb, defer=False):
    P_ = C.P
    ins = C.ins
    pre = {}
    Bw = pre["Bw"] = Buf("Cw")
    wout = pre["wout"] = sb("C_wout", [128, 8, D], BF16)
    wglu = pre["wglu"] = sb("C_wglu", [128, 4, DS], BF16)
    ones_bf = pre["ones_bf"] = sb("C_ones", [128, 128], BF16)
    gvec = pre["gvec"] = {}
    for nm in ("post_mix_g", "pre_mlp_g", "post_mlp_g"):
        gvec[nm] = sb("C_" + nm, [128, D], F32)
    pv_raw = pre["pv_raw"] = sb("C_pvraw", [4, 3, 128], F32)
    identf = pre["identf"] = sb("C_idf", [128, 128], F32)

    def issue():
        P_.dma("sp", wout[:], C.wb_out[layer].rearrange("(ct p) c -> p ct c", p=128), reads=[C.B_wb["out"]], writes=[Bw])
        P_.dma("sp", wglu[:], C.wb_glu[layer].rearrange("(ct p) c -> p ct c", p=128), reads=[C.B_wb["glu"]], writes=[Bw])
        P_.dma("sp", ones_bf[:], C.cins["ones_bf"].ap(), writes=[Bw])
        for nm in ("post_mix_g", "pre_mlp_g", "post_mlp_g"):
            P_.dma("sp", gvec[nm][:], ins[nm][layer].partition_broadcast(128), writes=[Bw])
        for k_, nm in enumerate(("fourier_out_g", "ssm_out_g", "b_glu")):
            P_.dma("sp", pv_raw[:, k_, :], ins[nm][layer].rearrange("(ct p) -> ct p", p=128), writes=[Bw])
        P_.dma("sp", identf[:], C.cins["ident_f32"].ap(), writes=[Bw])
    if defer:
        pre["issue"] = issue
    else:
        issue()
    return pre


def phase_C(C, layer, x_src, B_xsrc, x_dst, B_xdst, pre=None):
    P_ = C.P
    ins = C.ins
    ident_bf = C.ident_bf
    B_const = C.B_const
    st, sb, ps = _scope(C)

    def mm(out, lhsT, rhs, start, stop, R, W):
        P_.op("pe", lambda e: e.matmul(out, lhsT=lhsT, rhs=rhs, start=start, stop=stop), reads=R, writes=W)

    def tr(out, in_, ident, R, W):
        P_.op("pe", lambda e: e.transpose(out=out, in_=in_, identity=ident), reads=R, writes=W)

    def tt(out, in0, in1, op, R, W):
        P_.op("dve", lambda e: e.tensor_tensor(out=out, in0=in0, in1=in1, op=op), reads=R, writes=W)

    def stt(out, in0, scalar, in1, op0, op1, R, W):
        P_.op("dve", lambda e: e.scalar_tensor_tensor(out=out, in0=in0, scalar=scalar, in1=in1, op0=op0, op1=op1), reads=R, writes=W)

    def actf(out, in_, func, R, W, **kw):
        P_.op("act", lambda e: e.activation(out=out, in_=in_, func=func, **kw), reads=R, writes=W)

    def dcopy(out, in_, R, W):
        P_.op("dve", lambda e: e.tensor_copy(out=out, in_=in_), reads=R, writes=W)

    def recip(out, in_, R, W):
        P_.op("dve", lambda e: e.reciprocal(out=out, in_=in_), reads=R, writes=W)

    with st:
        if pre is None:
            pre = prefetch_C_consts(C, layer, sb)
        Bw, wout, wglu, ones_bf, gvec, pv_raw, identf = (pre[k] for k in ("Bw", "wout", "wglu", "ones_bf", "gvec", "pv_raw", "identf"))
        pv = sb("C_pv", [128, 3, 4], F32)
        epsb = sb("C_eps", [128, 1], F32)
        P_.op("pool", lambda e: e.memset(epsb[:], EPS), writes=[Bw])
        bank = [ps("C_bank%d" % i, [128, 512], F32) for i in range(8)]
        bankbf = [b_.bitcast(BF16) for b_ in bank]
        B_bank = [Buf("bank%d" % i) for i in range(8)]
        nb = [0]

        def getbank():
            i = nb[0] % 8
            nb[0] += 1
            return i
        bi = getbank()
        for k_ in range(3):
            tr(bank[bi][:, k_ * 4:(k_ + 1) * 4], pv_raw[:, k_, :], identf[0:4, 0:4], [Bw], [B_bank[bi]])
        dcopy(pv[:].rearrange("p a b -> p (a b)"), bank[bi][:, 0:12], [B_bank[bi]], [Bw])
        gf = pv[:, 0, :]
        gs = pv[:, 1, :]
        bgl = pv[:, 2, :]
        wbuf = [sb("C_wb%d" % i, [128, 8, 1024], BF16) for i in range(3)]
        B_wbuf = [Buf() for _ in range(3)]
        nw = [0]
        xt = sb("C_xt", [128, 4, D], F32)
        B_xt = Buf()
        yfT = sb("C_yfT", [128, 4, TT], F32)
        B_yfT = Buf()
        ysT = sb("C_ysT", [128, 4, DS], BF16)
        B_ysT = Buf()
        sq = sb("C_sq", [128, 4, TT], BF16)
        B_sq = Buf()
        rf = sb("C_rf", [128, TT], F32)
        B_rf = Buf()
        catT = sb("C_catT", [128, 8, TT], BF16)
        B_cat = Buf()
        gT = sb("C_gT", [128, 4, TT], BF16)
        B_gT = Buf()
        oT = sb("C_oT", [128, 4, TT], F32)
        B_oT = Buf()
        sg = sb("C_sg", [128, TT], BF16)
        B_sg = Buf()
        SC = [sb("C_SC%d" % i, [128, D], F32) for i in range(2)]
        B_SC = [Buf() for _ in range(2)]
        ssq = sb("C_ssq", [128, 8], F32)
        B_ssq = Buf()
        rs4 = sb("C_rs4", [128, 4], F32)
        B_rs4 = Buf()
        junk = sb("C_junk", [128, D], BF16)
        B_junk = Buf()
        h2 = sb("C_h2", [128, 4, D], BF16)
        B_h2 = Buf()
        h2T = sb("C_h2T", [128, 8, TT], BF16)
        B_h2T = Buf()
        aT = sb("C_aT", [128, 32, TT], BF16)
        B_aT = Buf()
        rl = [sb("C_rl%d" % i, [128, TT], F32) for i in range(2)]
        B_rl = [Buf() for _ in range(2)]
        nrl = [0]
        w1v = C.wb_ff1[layer].rearrange("(kt p) h -> p kt h", p=128)
        w2v = C.wb_ff2[layer].rearrange("(ht p) c -> p ht c", p=128)

        def wload(kind, c):
            i = nw[0] % 3
            nw[0] += 1
            if NOWLOAD and nw[0] > 3:
                return i
            if kind == 1:
                P_.dma("sp", wbuf[i][:], w1v[:, :, c * 1024:(c + 1) * 1024], reads=[C.B_wb["ff1"]], writes=[B_wbuf[i]])
            else:
                P_.dma("sp", wbuf[i][:], w2v[:, c * 8:(c + 1) * 8, :], reads=[C.B_wb["ff2"]], writes=[B_wbuf[i]])
            return i

        def rstd_from_sum(dst, src, n, R, W):
            actf(dst, src, AF.Sqrt, R, W, bias=epsb[:], scale=1.0 / n)
            recip(dst, dst, W, W)

        def residual_norm(pbanks, gname):
            g_ = gvec[gname]
            for s in range(4):
                for hf in range(2):
                    b_ = pbanks[s][hf]
                    actf(junk[:, 0:512], bank[b_][:], AF.Square, [B_bank[b_]], [B_junk, B_ssq], accum_out=ssq[:, s * 2 + hf:s * 2 + hf + 1])
            tt(rs4[:], ssq[:].rearrange("p (s h) -> p s h", h=2)[:, :, 0], ssq[:].rearrange("p (s h) -> p s h", h=2)[:, :, 1], ALU.add, [B_ssq], [B_rs4])
            P_.op("dve", lambda e: e.tensor_scalar(out=rs4[:], in0=rs4[:], scalar1=1.0 / D, scalar2=EPS, op0=ALU.mult, op1=ALU.add),
                  reads=[B_rs4], writes=[B_rs4])
            actf(rs4[:], rs4[:], AF.Sqrt, [B_rs4], [B_rs4])
            recip(rs4[:], rs4[:], [B_rs4], [B_rs4])
            for s in range(4):
                sc = s % 2
                for hf in range(2):
                    b_ = pbanks[s][hf]
                    actf(SC[sc][:, hf * 512:(hf + 1) * 512], bank[b_][:], AF.Copy, [B_bank[b_], B_rs4], [B_SC[sc]], scale=rs4[:, s:s + 1])
                tt(SC[sc][:], SC[sc][:], g_[:], ALU.mult, [B_SC[sc], Bw], [B_SC[sc]])
                tt(xt[:, s, :], xt[:, s, :], SC[sc][:], ALU.add, [B_SC[sc], B_xt], [B_xt])

        def tail_thunks(i):
            t0 = i * TT
            th = []
            st_ = {}

            def a1():
                P_.dma("sp", yfT[:], C.YF.ap().rearrange("(ct p) l -> p ct l", p=128)[:, :, t0:t0 + TT], reads=[C.B_YF], writes=[B_yfT])
                P_.dma("sp", ysT[:], C.YS[t0:t0 + TT, :].rearrange("(s p) c -> p s c", p=128), reads=[C.B_YS], writes=[B_ysT])
            th.append(a1)
            th.append(lambda: actf(sq[:], yfT[:], AF.Square, [B_yfT], [B_sq]))

            def a3():
                bi = getbank()
                st_["b1"] = bi
                for ct in range(4):
                    mm(bank[bi][:], ones_bf[:], sq[:, ct, :], ct == 0, ct == 3, [B_sq, Bw], [B_bank[bi]])
            th.append(a3)
            th.append(lambda: rstd_from_sum(rf[:], bank[st_["b1"]][:], DF, [B_bank[st_["b1"]], Bw], [B_rf]))

            def a5():
                for ct in range(4):
                    stt(catT[:, ct, :], yfT[:, ct, :], gf[:, ct:ct + 1], rf[:], ALU.mult, ALU.mult, [B_yfT, B_rf, Bw], [B_cat])
            th.append(a5)
            for s in range(4):
                def tr_s(s=s):
                    bi = getbank()
                    st_["t%d" % s] = bi
                    for ct in range(4):
                        tr(bankbf[bi][:, ct * 128:(ct + 1) * 128], ysT[:, s, ct * 128:(ct + 1) * 128], ident_bf[:], [B_ysT, B_const], [B_bank[bi]])
                th.append(tr_s)

                def ge_s(s=s):
                    bi = st_["t%d" % s]
                    actf(gT[:, :, s * 128:(s + 1) * 128], bankbf[bi][:, 0:512].rearrange("p (c t) -> p c t", c=4), AF.Gelu_apprx_tanh,
                         [B_bank[bi]], [B_gT])
                th.append(ge_s)
            for co in range(4):
                def glu_mm(co=co):
                    bi = getbank()
                    st_["g%d" % co] = bi
                    for ci in range(4):
                        mm(bank[bi][:], wglu[:, ci, co * 128:(co + 1) * 128], gT[:, ci, :], ci == 0, ci == 3, [Bw, B_gT], [B_bank[bi]])
                th.append(glu_mm)

                def glu_ev(co=co):
                    bi = st_["g%d" % co]
                    actf(sg[:], bank[bi][:], AF.Sigmoid, [B_bank[bi], Bw], [B_sg], bias=bgl[:, co:co + 1], scale=1.0)
                    tt(oT[:, co, :], gT[:, co, :], sg[:], ALU.mult, [B_gT, B_sg], [B_oT])
                th.append(glu_ev)
            th.append(lambda: actf(sq[:], oT[:], AF.Square, [B_oT], [B_sq]))

            def b3():
                bi = getbank()
                st_["b2"] = bi
                for ct in range(4):
                    mm(bank[bi][:], ones_bf[:], sq[:, ct, :], ct == 0, ct == 3, [B_sq, Bw], [B_bank[bi]])
            th.append(b3)
            th.append(lambda: rstd_from_sum(rf[:], bank[st_["b2"]][:], DS, [B_bank[st_["b2"]], Bw], [B_rf]))

            def b5():
                for ct in range(4):
                    stt(catT[:, 4 + ct, :], oT[:, ct, :], gs[:, ct:ct + 1], rf[:], ALU.mult, ALU.mult, [B_oT, B_rf, Bw], [B_cat])
            th.append(b5)
            return th

        wnext = wload(1, 0)
        for f_ in tail_thunks(0):
            f_()
        for i in range(NT):
            t0 = i * TT
            pending = tail_thunks(i + 1) if i + 1 < NT else []
            P_.dma("sp", xt[:], x_src[t0:t0 + TT, :].rearrange("(s p) d -> p s d", p=128), reads=[B_xsrc], writes=[B_xt])
            pb = [[None, None] for _ in range(4)]
            for s in range(4):
                for hf in range(2):
                    bi = getbank()
                    pb[s][hf] = bi
                    for ct in range(8):
                        mm(bank[bi][:], catT[:, ct, s * 128:(s + 1) * 128], wout[:, ct, hf * 512:(hf + 1) * 512], ct == 0, ct == 7,
                           [B_cat, Bw], [B_bank[bi]])
            residual_norm(pb, "post_mix_g")
            for s in range(4):
                actf(junk[:], xt[:, s, :], AF.Square, [B_xt], [B_junk, B_ssq], accum_out=ssq[:, s:s + 1])
            P_.op("dve", lambda e: e.tensor_scalar(out=rs4[:], in0=ssq[:, 0:4], scalar1=1.0 / D, scalar2=EPS, op0=ALU.mult, op1=ALU.add),
                  reads=[B_ssq], writes=[B_rs4])
            actf(rs4[:], rs4[:], AF.Sqrt, [B_rs4], [B_rs4])
            recip(rs4[:], rs4[:], [B_rs4], [B_rs4])
            for s in range(4):
                stt(h2[:, s, :], xt[:, s, :], rs4[:, s:s + 1], gvec["pre_mlp_g"][:], ALU.mult, ALU.mult, [B_xt, B_rs4, Bw], [B_h2])
            for s in range(4):
                bi = getbank()
                for kt in range(8):
                    tr(bankbf[bi][:, kt * 128:(kt + 1) * 128], h2[:, s, kt * 128:(kt + 1) * 128], ident_bf[:], [B_h2, B_const], [B_bank[bi]])
                src = bankbf[bi][:].rearrange("p (k t) -> p k t", k=8)
                if s % 2 == 0:
                    P_.op("act", lambda e, s=s, src=src: e.copy(out=h2T[:, :, s * 128:(s + 1) * 128], in_=src), reads=[B_bank[bi]], writes=[B_h2T])
                else:
                    dcopy(h2T[:, :, s * 128:(s + 1) * 128], src, [B_bank[bi]], [B_h2T])
            for c in range(4):
                wi_ = wnext
                wnext = wload(1, c + 1) if c < 3 else wload(2, 0)
                for hl in range(8):
                    ht = c * 8 + hl
                    bi = getbank()
                    for kt in range(8):
                        mm(bank[bi][:], wbuf[wi_][:, kt, hl * 128:(hl + 1) * 128], h2T[:, kt, :], kt == 0, kt == 7, [B_wbuf[wi_], B_h2T], [B_bank[bi]])
                    r_ = nrl[0] % 2
                    nrl[0] += 1
                    actf(rl[r_][:], bank[bi][:], AF.Relu, [B_bank[bi]], [B_rl[r_]])
                    if ht % 2 == 0:
                        tt(aT[:, ht, :], rl[r_][:], rl[r_][:], ALU.mult, [B_rl[r_]], [B_aT])
                    else:
                        actf(aT[:, ht, :], rl[r_][:], AF.Square, [B_rl[r_]], [B_aT])
                    if pending and ht >= 2:
                        pending.pop(0)()
            while pending:
                pending.pop(0)()
            pb = [[s * 2 + hf for hf in range(2)] for s in range(4)]
            nb[0] = 0
            for c in range(4):
                wi_ = wnext
                if c < 3:
                    wnext = wload(2, c + 1)
                elif i + 1 < NT:
                    wnext = wload(1, 0)
                for s in range(4):
                    for hf in range(2):
                        bi = pb[s][hf]
                        for hl in range(8):
                            ht = c * 8 + hl
                            mm(bank[bi][:], aT[:, ht, s * 128:(s + 1) * 128], wbuf[wi_][:, hl, hf * 512:(hf + 1) * 512],
                               (c == 0 and hl == 0), (c == 3 and hl == 7), [B_aT, B_wbuf[wi_]], [B_bank[bi]])
            residual_norm(pb, "post_mlp_g")
            P_.dma("sp", x_dst[t0:t0 + TT, :].rearrange("(s p) d -> p s d", p=128), xt[:], reads=[B_xt], writes=[B_xdst])
        P_.barrier()


def build(debug=None, stop_after=None, bs_level=9, skipA=False):
    debug = debug or set()
    nc = bass.Bass("TRN2", target_bir_lowering=False)
    C = Ctx()
    C.nc = nc
    C.P = P_ = Prog(nc)
    dt = nc.dram_tensor

    def kind(nm):
        return "ExternalOutput" if nm in debug else "Internal"

    x_in = dt("x", [LO, D], F32, kind="ExternalInput")
    C.ins = {}
    shapes = {
        "w_in": [DEPTH, D, D], "w_out": [DEPTH, D, D],
        "pre_mix_g": [DEPTH, D], "post_mix_g": [DEPTH, D], "pre_mlp_g": [DEPTH, D], "post_mlp_g": [DEPTH, D],
        "fourier_out_g": [DEPTH, DF], "ssm_out_g": [DEPTH, DS],
        "w_fourier": [DEPTH, NH, 64, 64],
        "lam_re": [DEPTH, 2, G, P], "lam_im": [DEPTH, 2, G, P], "log_dt": [DEPTH, 2, G],
        "b_re": [DEPTH, 2, G, P, H], "b_im": [DEPTH, 2, G, P, H],
        "c_re": [DEPTH, 2, G, H, P], "c_im": [DEPTH, 2, G, H, P],
        "d_skip": [DEPTH, G, H], "w_glu": [DEPTH, DS, DS], "b_glu": [DEPTH, DS],
        "w_ff1": [DEPTH, D, DFF], "w_ff2": [DEPTH, DFF, D],
    }
    for k, s in shapes.items():
        C.ins[k] = dt(k, s, F32, kind="ExternalInput")
    consts = _consts(0)
    C.cins = {}
    for k, v in consts.items():
        C.cins[k] = dt(k, list(v.shape), BF16 if v.dtype == ml_dtypes.bfloat16 else F32, kind="ExternalInput")
    y_out = dt("y", [LO, D], F32, kind="ExternalOutput")

    C.wb_in = dt("wb_in", [DEPTH, D, D], BF16)
    C.wb_out = dt("wb_out", [DEPTH, D, D], BF16)
    C.wb_glu = dt("wb_glu", [DEPTH, DS, DS], BF16)
    C.wb_ff1 = dt("wb_ff1", [DEPTH, D, DFF], BF16)
    C.wb_ff2 = dt("wb_ff2", [DEPTH, DFF, D], BF16)
    C.Zown = dt("zown", [LO, D], BF16)
    C.Zs = dt("zs", [L, D], BF16, addr_space="Local")
    C.YF = dt("yf", [DF, LO], F32, kind=kind("yf"))
    C.YS = dt("ysm", [LO, DS], BF16, kind=kind("ysm"))
    C.X1 = dt("x1", [LO, D], F32, kind=kind("x1"))
    C.B_Zown = Buf("Zown")
    C.B_Zg = [Buf("Zg%d" % i) for i in range(4)]
    C.B_wb = {k: Buf(k) for k in ("in", "out", "glu", "ff1", "ff2")}
    C.B_Z = Buf("Z")
    C.B_YF = Buf("YF")
    C.B_YS = Buf("YS")
    C.dbg = dt("dbg", [128, 8192], F32, kind="ExternalOutput") if "dbg" in debug else None
    C.dbgU = dt("dbgU", [128, 4, 1024], BF16, kind="ExternalOutput") if "dbg" in debug else None
    C.dbgB = dt("dbgB", [128, 8192], BF16, kind="ExternalOutput") if "dbg" in debug else None
    C.dbgF = dt("dbgF", [128, 4096], F32, kind="ExternalOutput") if "dbg" in debug else None
    C.dbg_map = {}
    C.bs_level = bs_level

    st, sb, ps = _scope(C)
    with st:
        def cast_weights(which):
            for (i, key) in which:
                if key in ("in", "out", "glu"):
                    src, dst = {"in": (C.ins["w_in"], C.wb_in), "out": (C.ins["w_out"], C.wb_out), "glu": (C.ins["w_glu"], C.wb_glu)}[key]
                    P_.dma("pool", dst[i], src[i], writes=[C.B_wb[key]])
                elif key == "ff1":
                    for q in range(4):
                        P_.dma("pool", C.wb_ff1[i, q * 256:(q + 1) * 256, :].rearrange("r (a c) -> (r a) c", c=1024),
                               C.ins["w_ff1"][i, q * 256:(q + 1) * 256, :].rearrange("r (a c) -> (r a) c", c=1024),
                               writes=[C.B_wb["ff1"]])
                else:
                    for q in range(4):
                        P_.dma("pool", C.wb_ff2[i, q * 1024:(q + 1) * 1024, :], C.ins["w_ff2"][i, q * 1024:(q + 1) * 1024, :],
                               writes=[C.B_wb["ff2"]])
        C.ident_bf = sb("ident_bf", [128, 128], BF16)
        C.B_const = Buf("const")
        P_.dma("sp", C.ident_bf[:], C.cins["ident_bf"].ap(), writes=[C.B_const])

        B_x = Buf("x")
        B_x1 = Buf("x1")
        B_y = Buf("y")
        srcs = [(x_in, B_x, C.X1, B_x1), (C.X1, B_x1, y_out, B_y)]
        for layer in range(DEPTH if stop_after is None else 1):
            xs, Bxs, xd, Bxd = srcs[layer]
            if not skipA:
                phase_A(C, layer, xs, Bxs)
            if layer == 0:
                cast_weights([(0, "out"), (0, "glu"), (0, "ff1"), (0, "ff2")])
            if stop_after in ("BS", "C", None):
                phase_BS(C, layer)
            if layer == 0:
                cast_weights([(1, "out"), (1, "glu"), (1, "ff1"), (1, "ff2")])
            stc, sbc, psc = _scope(C)
            with stc:
                pre = prefetch_C_consts(C, layer, sbc, defer=True) if stop_after in ("C", None) else None
                if stop_after in ("BF", "C", None) and not skipA:
                    phase_BF(C, layer, after_loads=pre["issue"] if pre else None)
                elif pre:
                    pre["issue"]()
                if stop_after in ("C", None):
                    phase_C(C, layer, xs, Bxs, xd, Bxd, pre=pre)
        P_.barrier()
        P_.emit()
    C_last[0] = C
    return nc, consts


C_last = [None]


def _run(inputs, debug=None, n_cores=8, stop_after=None, bs_level=9, trace=False):
    nc, _ = build(debug, stop_after, bs_level)
    in_maps = []
    x = np.ascontiguousarray(inputs["x"], dtype=np.float32)
    shared = {k: np.ascontiguousarray(v, dtype=np.float32) for k, v in inputs.items() if k != "x"}
    cst = [_consts(0), _consts(1)]
    for c in range(n_cores):
        b, r = c // 2, c % 2
        m = {"x": np.ascontiguousarray(x[b].reshape(4, 2, 1024, D)[:, r].reshape(LO, D))}
        m.update(shared)
        m.update(cst[r])
        in_maps.append(m)
    if trace:
        return run_bass_kernel_spmd(nc, in_maps, core_ids=list(range(n_cores)), trace=True)
    return run_bass_kernel_spmd(nc, in_maps, core_ids=list(range(n_cores)))


def kernel(**inputs):
    res = _run(inputs, n_cores=8)
    out = np.empty((4, L, D), dtype=np.float32)
    for c in range(8):
        b, r = c // 2, c % 2
        out[b].reshape(4, 2, 1024, D)[:, r] = np.asarray(res.results[c]["y"], dtype=np.float32).reshape(4, 1024, D)
    return out
```

```python
import math
import numpy as np
import ml_dtypes
import concourse.bass as bass
import concourse.mybir as mybir
from concourse.bass_utils import run_bass_kernel_spmd

F32 = mybir.dt.float32
BF16 = mybir.dt.bfloat16
AF = mybir.ActivationFunctionType
ALU = mybir.AluOpType

D = 1024
L = 8192
DEPTH = 2
DF = 512
DS = 512
NH = 8
G = 32
H = 16
P = 64
DFF = 4096
EPS = 1e-6
TT = 512
LO = L // 2
NT = LO // TT


class Buf:
    __slots__ = ("w", "r", "name")

    def __init__(self, name=""):
        self.w = None
        self.r = {}
        self.name = name


class Prog:
    ENG = ("pe", "act", "dve", "pool", "sp")

    def __init__(self, nc, n_dma_sems=48):
        self.nc = nc
        self.lists = {e: [] for e in self.ENG}
        self.cnt = {e: 0 for e in self.ENG}
        self.waited = {e: {} for e in self.ENG}
        self.n_dma = n_dma_sems
        self.dma_val = [0] * n_dma_sems
        self.dma_next = 0
        self.n_sw = 8
        self.n_cc = 8
        self.cc_next = 0
        self.sw_next = 0
        self.sems = {}
        self.keys = set()

    def _deps(self, eng, reads, writes):
        ev = {}

        def add(e):
            if e is None:
                return
            k, v = e
            if ev.get(k, 0) < v:
                ev[k] = v

        for b in reads:
            add(b.w)
        for b in writes:
            if not (eng == "pe" and b.w is not None and b.w[0][0] == "pe"):
                add(b.w)
            for k, v in b.r.items():
                if not (k[0] == "pe" and eng == "pe"):
                    add((k, v))
        wl = self.waited[eng]
        for k, v in ev.items():
            if wl.get(k, 0) < v:
                wl[k] = v
                self.lists[eng].append(("wait", k, v))

    def _commit(self, e, reads, writes):
        k, v = e
        for b in reads:
            if b.r.get(k, 0) < v:
                b.r[k] = v
        for b in writes:
            b.w = e
            b.r = {}

    EP = 2000

    def op(self, eng, fn, reads=(), writes=()):
        self._deps(eng, reads, writes)
        n = self.cnt[eng]
        self.cnt[eng] += 1
        key = (eng, n // self.EP)
        e = (key, n % self.EP + 1)
        self.keys.add(key)
        self.lists[eng].append(("op", fn, key, 1))
        self._commit(e, reads, writes)
        return e

    def dma(self, q, out, in_, reads=(), writes=(), custom=None, inc=16, **kw):
        if custom is not None:
            s = self.n_dma - self.n_sw - 1 - self.cc_next
            self.cc_next += 1
            assert self.cc_next <= self.n_cc
        elif q == "pool":
            s = self.n_dma - self.n_sw + self.sw_next
            self.sw_next = (self.sw_next + 1) % self.n_sw
        else:
            s = self.dma_next
            self.dma_next = (s + 1) % (self.n_dma - self.n_sw - self.n_cc)
        key = ("dma", s)
        if self.dma_val[s] > 0:
            wl = self.waited[q]
            if wl.get(key, 0) < self.dma_val[s]:
                wl[key] = self.dma_val[s]
                self.lists[q].append(("wait", key, self.dma_val[s]))
        self._deps(q, reads, writes)
        self.dma_val[s] += inc
        e = (key, self.dma_val[s])

        def fn(eng, out=out, in_=in_, kw=kw):
            if custom is not None:
                return custom(eng)
            return eng.dma_start(out=out, in_=in_, **kw)

        self.lists[q].append(("op", fn, key, inc))
        self._commit(e, reads, writes)
        return e

    def wait_event(self, eng, e):
        k, v = e
        wl = self.waited[eng]
        if wl.get(k, 0) < v:
            wl[k] = v
            self.lists[eng].append(("wait", k, v))

    def barrier(self, skip_cc=False):
        evs = []
        for e in self.ENG:
            n = self.cnt[e]
            if n > 0:
                ep = (n - 1) // self.EP
                evs.append(((e, ep), (n - 1) % self.EP + 1))
                if ep > 0:
                    evs.append(((e, ep - 1), self.EP))
        cc_slots = set(range(self.n_dma - self.n_sw - self.n_cc, self.n_dma - self.n_sw)) if skip_cc else set()
        evs += [(("dma", s), self.dma_val[s]) for s in range(self.n_dma) if self.dma_val[s] > 0 and s not in cc_slots]
        for eng in self.ENG:
            for e in evs:
                self.wait_event(eng, e)

    def emit(self, final_events=()):
        nc = self.nc
        import contextlib
        with contextlib.ExitStack() as st:
            sem = {}
            for key in sorted(self.keys):
                sem[key] = st.enter_context(nc.semaphore("s_%s_%d" % key))
            for s in range(self.n_dma):
                sem[("dma", s)] = st.enter_context(nc.semaphore("d%d" % s))
            block = st.enter_context(nc.Block())
            lists = self.lists

            def run(engobj, items):
                for it in items:
                    if it[0] == "wait":
                        engobj.wait_ge(sem[it[1]], it[2])
                    else:
                        ins = it[1](engobj)
                        ins.then_inc(sem[it[2]], it[3])

            @block.tensor
            def _(e):
                run(e, lists["pe"])

            @block.scalar
            def _(e):
                run(e, lists["act"])

            @block.vector
            def _(e):
                run(e, lists["dve"])

            @block.gpsimd
            def _(e):
                run(e, lists["pool"])

            @block.sync
            def _(e):
                run(e, lists["sp"])


def _bf(a):
    return np.ascontiguousarray(a, dtype=np.float32).astype(ml_dtypes.bfloat16)


def _consts(rank=0):
    c = {}
    c["ident_bf"] = _bf(np.eye(128))
    c["ident_f32"] = np.eye(128, dtype=np.float32)
    c["ones_bf"] = _bf(np.ones((128, 128)))
    l1 = np.arange(128)[:, None].astype(np.float64)
    k1 = np.arange(128)[None, :].astype(np.float64)
    ang = 2 * np.pi * l1 * k1 / 128.0
    s1 = 1.0 / np.sqrt(128.0)
    w1 = np.stack([np.cos(ang) * s1, -np.sin(ang) * s1], axis=-1)
    c["dft_w1"] = _bf(w1.reshape(128, 256))
    l2 = np.arange(64).astype(np.float64)
    k2 = np.arange(64).astype(np.float64)
    m2 = np.zeros((2, 64, 64, 2, 32, 2), dtype=np.float64)
    k2 = np.array([16 * c_ + 8 * rank + m_ for c_ in range(4) for m_ in range(8)], dtype=np.float64)
    s2 = 1.0 / np.sqrt(64.0)
    for b in range(2):
        k1v = (64 * b + np.arange(64)).astype(np.float64)
        kk = k1v[None, :, None] + 128.0 * k2[None, None, :]
        th = 2 * np.pi * kk * l2[:, None, None] / 8192.0
        mr = np.cos(th) * s2
        mi = -np.sin(th) * s2
        m2[b, :, :, 0, :, 0] = mr
        m2[b, :, :, 0, :, 1] = mi
        m2[b, :, :, 1, :, 0] = -mi
        m2[b, :, :, 1, :, 1] = mr
    c["dft_m2"] = _bf(m2.reshape(128, 64 * 2 * 64))
    msk = np.zeros((128, 2), dtype=np.float32)
    msk[:, rank] = 1.0
    c["s5_sel"] = msk
    cc = np.arange(64)[:, None].astype(np.float64)
    dd = np.arange(64)[None, :].astype(np.float64)
    a64 = 2 * np.pi * cc * dd / 64.0
    cs = np.zeros((64, 2, 128), dtype=np.float32)
    cs[:, 0, :64] = np.cos(a64) / 8.0
    cs[:, 0, 64:] = np.cos(a64) / 8.0
    cs[:, 1, :64] = np.sin(a64) / 8.0
    cs[:, 1, 64:] = np.sin(a64) / 8.0
    c["dft_cs"] = cs.reshape(64, 256)
    n = np.arange(40, dtype=np.float32)
    ev = np.zeros((2, 2, 40), dtype=np.float32)
    ev[0, 0] = 31.0 - n
    ev[0, 1] = n
    ev[1, 0] = n
    ev[1, 1] = 32.0 - n
    c["s5_ev"] = np.ascontiguousarray(np.broadcast_to(ev.reshape(1, 160), (128, 160)))
    tau = np.arange(128) // 16
    c["s5_maskf"] = (tau[None, :] >= tau[:, None]).astype(np.float32)
    c["s5_maskb"] = (tau[:, None] >= tau[None, :]).astype(np.float32)
    return c


class Ctx:
    pass


_uid = [0]


def _scope(C):
    import contextlib
    st = contextlib.ExitStack()
    nc = C.nc

    def sb(name, shape, dtype):
        _uid[0] += 1
        return st.enter_context(nc.sbuf_tensor("sb%d_%s" % (_uid[0], name), shape, dtype))

    def ps(name, shape, dtype):
        _uid[0] += 1
        return st.enter_context(nc.psum_tensor("ps%d_%s" % (_uid[0], name), shape, dtype))
    return st, sb, ps


def phase_A(C, layer, x_src, B_xsrc):
    P_ = C.P
    st, sb, ps = _scope(C)
    with st:
        ident_bf = C.ident_bf
        B_const = C.B_const
        g_pre = sb("A_g", [128, D], F32)
        B_g = Buf()
        P_.dma("sp", g_pre[:], C.ins["pre_mix_g"][layer].partition_broadcast(128), writes=[B_g])
        win = sb("A_win", [128, 8, D], BF16)
        B_win = Buf("win")
        winf = sb("A_winf", [128, 8, D], F32)
        B_winf = Buf()
        P_.dma("sp", winf[:], C.ins["w_in"][layer].rearrange("(kt p) c -> p kt c", p=128), writes=[B_winf])
        for kt in range(8):
            if kt % 2 == 0:
                P_.op("act", lambda e, kt=kt: e.copy(out=win[:, kt, :], in_=winf[:, kt, :]), reads=[B_winf], writes=[B_win])
            else:
                P_.op("dve", lambda e, kt=kt: e.tensor_copy(out=win[:, kt, :], in_=winf[:, kt, :]), reads=[B_winf], writes=[B_win])
        xt = [sb("A_xt%d" % i, [128, 4, D], F32) for i in range(2)]
        B_xt = [Buf() for _ in range(2)]
        junk = sb("A_junk", [128, D], BF16)
        B_junk = Buf()
        ss = [sb("A_ss%d" % i, [128, 4], F32) for i in range(2)]
        B_ss = [Buf() for _ in range(2)]
        rs = [sb("A_rs%d" % i, [128, 4], F32) for i in range(2)]
        B_rs = [Buf() for _ in range(2)]
        hb = [sb("A_hb%d" % i, [128, 4, D], BF16) for i in range(2)]
        B_hb = [Buf() for _ in range(2)]
        hT = [sb("A_hT%d" % i, [128, 8, TT], BF16) for i in range(2)]
        B_hT = [Buf() for _ in range(2)]
        zt = [sb("A_zt%d" % i, [128, 4, D], BF16) for i in range(2)]
        B_zt = [Buf() for _ in range(2)]
        pT = [ps("A_pT%d" % i, [128, 8, 128], BF16) for i in range(2)]
        B_pT = [Buf() for _ in range(2)]
        pz = [ps("A_pz%d" % i, [128, 512], F32) for i in range(4)]
        B_pz = [Buf() for _ in range(4)]

        B_Zc = [Buf() for _ in range(4)]

        def load(i):
            b = i % 2
            P_.dma("sp", xt[b][:], x_src[i * TT:(i + 1) * TT, :].rearrange("(s p) d -> p s d", p=128),
                   reads=[B_xsrc], writes=[B_xt[b]])
        load(0)
        cnt = {"npz": 0, "npt": 0}

        def front(i):
            b = i % 2
            if i + 1 < NT:
                load(i + 1)
            for s in range(4):
                P_.op("act", lambda e, b=b, s=s: e.activation(out=junk[:], in_=xt[b][:, s, :], func=AF.Square,
                                                         accum_out=ss[b][:, s:s + 1]),
                      reads=[B_xt[b]], writes=[B_junk, B_ss[b]])
            P_.op("dve", lambda e, b=b: e.tensor_scalar(out=rs[b][:], in0=ss[b][:], scalar1=1.0 / D, scalar2=EPS,
                                                        op0=ALU.mult, op1=ALU.add), reads=[B_ss[b]], writes=[B_rs[b]])
            P_.op("act", lambda e, b=b: e.activation(out=rs[b][:], in_=rs[b][:], func=AF.Sqrt), reads=[B_rs[b]], writes=[B_rs[b]])
            P_.op("dve", lambda e, b=b: e.reciprocal(out=rs[b][:], in_=rs[b][:]), reads=[B_rs[b]], writes=[B_rs[b]])
            for s in range(4):
                P_.op("dve", lambda e, b=b, s=s: e.scalar_tensor_tensor(
                    out=hb[b][:, s, :], in0=xt[b][:, s, :], scalar=rs[b][:, s:s + 1], in1=g_pre[:],
                    op0=ALU.mult, op1=ALU.mult), reads=[B_xt[b], B_rs[b], B_g], writes=[B_hb[b]])

        def back(i):
            b = i % 2
            for s in range(4):
                tb = cnt["npt"] % 2
                cnt["npt"] += 1
                for kt in range(8):
                    P_.op("pe", lambda e, b=b, s=s, kt=kt, tb=tb: e.transpose(
                        out=pT[tb][:, kt, :], in_=hb[b][:, s, kt * 128:(kt + 1) * 128], identity=ident_bf[:]),
                        reads=[B_hb[b], B_const], writes=[B_pT[tb]])
                if s % 2 == 0:
                    P_.op("act", lambda e, b=b, s=s, tb=tb: e.copy(out=hT[b][:, :, s * 128:(s + 1) * 128], in_=pT[tb][:]),
                          reads=[B_pT[tb]], writes=[B_hT[b]])
                else:
                    P_.op("dve", lambda e, b=b, s=s, tb=tb: e.tensor_copy(out=hT[b][:, :, s * 128:(s + 1) * 128], in_=pT[tb][:]),
                          reads=[B_pT[tb]], writes=[B_hT[b]])
            for s in range(4):
                for hf in range(2):
                    zb = cnt["npz"] % 4
                    cnt["npz"] += 1
                    for kt in range(8):
                        P_.op("pe", lambda e, b=b, s=s, hf=hf, kt=kt, zb=zb: e.matmul(
                            pz[zb][:], lhsT=hT[b][:, kt, s * 128:(s + 1) * 128], rhs=win[:, kt, hf * 512:(hf + 1) * 512],
                            start=(kt == 0), stop=(kt == 7)), reads=[B_hT[b], B_win], writes=[B_pz[zb]])
                    if hf == 0:
                        P_.op("act", lambda e, b=b, s=s, hf=hf, zb=zb: e.copy(out=zt[b][:, s, hf * 512:(hf + 1) * 512], in_=pz[zb][:]),
                              reads=[B_pz[zb]], writes=[B_zt[b]])
                    else:
                        P_.op("dve", lambda e, b=b, s=s, hf=hf, zb=zb: e.tensor_copy(out=zt[b][:, s, hf * 512:(hf + 1) * 512], in_=pz[zb][:]),
                              reads=[B_pz[zb]], writes=[B_zt[b]])
            P_.dma("sp", C.Zown[i * TT:(i + 1) * TT, :].rearrange("(s p) d -> p s d", p=128), zt[b][:],
                   reads=[B_zt[b]], writes=[C.B_Zown, B_Zc[i // 2]])
            if i % 2 == 1:
                c_ = i // 2
                P_.dma("pool", None, None, reads=[B_Zc[c_]], writes=[C.B_Z, C.B_Zg[c_]], inc=1,
                       custom=lambda e, c_=c_: e.collective_compute(
                           "AllGather", ALU.bypass, replica_groups=[[0, 1], [2, 3], [4, 5], [6, 7]],
                           ins=[C.Zown[c_ * 1024:(c_ + 1) * 1024, :]], outs=[C.Zs[c_ * 2048:(c_ + 1) * 2048, :]]))

        front(0)
        for i in range(NT):
            if i + 1 < NT:
                front(i + 1)
            back(i)
        P_.barrier(skip_cc=True)

def phase_BF(C, layer, after_loads=None):
    P_ = C.P
    st, sb, ps = _scope(C)
    with st:
        B_c = Buf()
        w1 = sb("F_w1", [128, 256], BF16)
        P_.dma("sp", w1[:], C.cins["dft_w1"].ap(), writes=[B_c])
        m2 = sb("F_m2", [128, 64, 2, 64], BF16)
        P_.dma("sp", m2[:], C.cins["dft_m2"].ap().rearrange("p (k r n) -> p k r n", k=64, r=2), writes=[B_c])
        cs = sb("F_cs", [64, 2, 128], F32)
        P_.dma("sp", cs[:], C.cins["dft_cs"].ap().rearrange("p (r n) -> p r n", r=2), writes=[B_c])
        wf = sb("F_wf", [64, NH, 64], F32)
        P_.dma("sp", wf[:], C.ins["w_fourier"][layer].rearrange("h d e -> d h e"), writes=[B_c])
        wf2 = sb("F_wf2", [128, 4, 2, 128], BF16)
        B_wf2 = Buf()
        P_.op("pool", lambda e: e.memset(wf2[:], 0.0), writes=[B_wf2])
        psw = ps("F_psw", [128, 16, 64], F32)
        B_psw = Buf()
        for h in range(NH):
            for ri in range(2):
                P_.op("pe", lambda e, h=h, ri=ri: e.matmul(psw[:, h * 2 + ri, :], lhsT=cs[:, ri, :], rhs=wf[:, h, :], start=True, stop=True),
                      reads=[B_c], writes=[B_psw])
        for h in range(NH):
            hp, a = h // 2, h % 2
            for ri in range(2):
                P_.op("dve", lambda e, h=h, ri=ri, hp=hp, a=a: e.tensor_copy(
                    out=wf2[64 * a:64 * a + 64, hp, ri, 64 * a:64 * a + 64], in_=psw[64 * a:64 * a + 64, h * 2 + ri, :]),
                    reads=[B_psw], writes=[B_wf2])

        zf = [sb("F_zf%d" % i, [128, 64, 128], BF16) for i in range(2)]
        B_zf = [Buf() for _ in range(2)]
        a2 = sb("F_a2", [128, 64, 2, 128], BF16)
        B_a2 = Buf()
        gt = sb("F_gt", [128, 2, LO], BF16)
        B_gt = Buf()
        ys = [sb("F_ys%d" % i, [128, 2048], F32) for i in range(2)]
        B_ys = [Buf() for _ in range(2)]
        p1 = [ps("F_p1%d" % i, [128, 4, 128], F32) for i in range(2)]
        B_p1 = [Buf() for _ in range(2)]
        p2 = [ps("F_p2%d" % i, [128, 4, 64], F32) for i in range(2)]
        B_p2 = [Buf() for _ in range(2)]
        p3 = [ps("F_p3%d" % i, [128, 512], F32) for i in range(2)]
        B_p3 = [Buf() for _ in range(2)]
        Zv = C.Zs.ap().rearrange("(l1 l2) c -> l1 l2 c", l2=64)

        def load(hp):
            P_.dma("sp", zf[hp % 2][:], Zv[:, :, hp * 128:(hp + 1) * 128], reads=C.B_Zg, writes=[B_zf[hp % 2]])
        load(0)
        if after_loads is not None:
            after_loads()
        n1 = n2 = n3 = 0
        for hp in range(4):
            zb = hp % 2
            if hp + 1 < 4:
                load(hp + 1)
            for c4 in range(32):
                pb = n1 % 2
                n1 += 1
                for cc in range(4):
                    c = c4 * 4 + cc
                    for b in range(2):
                        P_.op("pe", lambda e, zb=zb, c=c, cc=cc, b=b, pb=pb: e.matmul(
                            p1[pb][64 * b:64 * b + 64, cc, :], lhsT=zf[zb][:, :, c], rhs=w1[:, b * 128:(b + 1) * 128],
                            start=True, stop=True), reads=[B_zf[zb], B_c], writes=[B_p1[pb]])
                outap = a2[:, :, :, c4 * 4:(c4 + 1) * 4].rearrange("p k r c -> p c k r")
                inap = p1[pb][:].rearrange("p c (k r) -> p c k r", r=2)
                if c4 % 2 == 0:
                    P_.op("act", lambda e, outap=outap, inap=inap: e.copy(out=outap, in_=inap), reads=[B_p1[pb]], writes=[B_a2])
                else:
                    P_.op("dve", lambda e, outap=outap, inap=inap: e.tensor_copy(out=outap, in_=inap), reads=[B_p1[pb]], writes=[B_a2])
            gtv = gt[:].rearrange("c r (k2 k1) -> c k1 k2 r", k1=128)
            for q in range(32):
                pb = n2 % 2
                n2 += 1
                for kk in range(4):
                    k1 = q * 4 + kk
                    b, k1l = k1 // 64, k1 % 64
                    for ri in range(2):
                        P_.op("pe", lambda e, b=b, k1l=k1l, ri=ri, kk=kk, pb=pb: e.matmul(
                            p2[pb][:, kk, :], lhsT=a2[64 * b:64 * b + 64, k1l, ri, :], rhs=m2[64 * b:64 * b + 64, k1l, ri, :],
                            start=(ri == 0), stop=(ri == 1)), reads=[B_a2, B_c], writes=[B_p2[pb]])
                outap = gtv[:, q * 4:(q + 1) * 4, :, :]
                inap = p2[pb][:].rearrange("c k (k2 r) -> c k k2 r", r=2)
                if q % 2 == 0:
                    P_.op("act", lambda e, outap=outap, inap=inap: e.copy(out=outap, in_=inap), reads=[B_p2[pb]], writes=[B_gt])
                else:
                    P_.op("dve", lambda e, outap=outap, inap=inap: e.tensor_copy(out=outap, in_=inap), reads=[B_p2[pb]], writes=[B_gt])
            for tb in range(LO // 512):
                pb = n3 % 2
                n3 += 1
                yb = (hp * (LO // 2048) + tb // 4) % 2
                for ri in range(2):
                    P_.op("pe", lambda e, hp=hp, ri=ri, tb=tb, pb=pb: e.matmul(
                        p3[pb][:], lhsT=wf2[:, hp, ri, :], rhs=gt[:, ri, tb * 512:(tb + 1) * 512],
                        start=(ri == 0), stop=(ri == 1)), reads=[B_wf2, B_gt], writes=[B_p3[pb]])
                if tb % 2 == 0:
                    P_.op("act", lambda e, yb=yb, tb=tb, pb=pb: e.copy(out=ys[yb][:, (tb % 4) * 512:(tb % 4 + 1) * 512], in_=p3[pb][:]),
                          reads=[B_p3[pb]], writes=[B_ys[yb]])
                else:
                    P_.op("dve", lambda e, yb=yb, tb=tb, pb=pb: e.tensor_copy(out=ys[yb][:, (tb % 4) * 512:(tb % 4 + 1) * 512], in_=p3[pb][:]),
                          reads=[B_p3[pb]], writes=[B_ys[yb]])
                if tb % 4 == 3:
                    t0 = (tb // 4) * 2048
                    P_.dma("sp", C.YF[hp * 128:(hp + 1) * 128, t0:t0 + 2048], ys[yb][:], reads=[B_ys[yb]], writes=[C.B_YF])
        P_.barrier()


MAGIC = 12582912.0
TWO_PI = 2.0 * math.pi
CW1 = float(np.float32(6.28125))
CW2 = float(np.float32(TWO_PI - 6.28125))
CW3 = float(TWO_PI - CW1 - CW2)
JC = L // 32
POOL_TABLES = True
NOWLOAD = False
BS_PIPE = True


def phase_BS(C, layer):
    P_ = C.P
    nc = C.nc
    ins = C.ins
    ident_bf = C.ident_bf
    B_const = C.B_const
    st, sb, ps = _scope(C)

    def tt(eng, out, in0, in1, op, R, W):
        P_.op(eng, lambda e: e.tensor_tensor(out=out, in0=in0, in1=in1, op=op), reads=R, writes=W)

    def ts(eng, out, in0, s1, s2, op0, op1, R, W):
        if op1 is None:
            P_.op(eng, lambda e: e.tensor_scalar(out=out, in0=in0, scalar1=s1, scalar2=None, op0=op0), reads=R, writes=W)
        else:
            P_.op(eng, lambda e: e.tensor_scalar(out=out, in0=in0, scalar1=s1, scalar2=s2, op0=op0, op1=op1), reads=R, writes=W)

    def stt(out, in0, scalar, in1, op0, op1, R, W):
        P_.op("dve", lambda e: e.scalar_tensor_tensor(out=out, in0=in0, scalar=scalar, in1=in1, op0=op0, op1=op1), reads=R, writes=W)

    def actf(out, in_, func, R, W, **kw):
        P_.op("act", lambda e: e.activation(out=out, in_=in_, func=func, **kw), reads=R, writes=W)

    def acopy(out, in_, R, W):
        P_.op("act", lambda e: e.copy(out=out, in_=in_), reads=R, writes=W)

    with st:
        U_all = sb("S_U", [128, G, 1024], BF16)
        B_U = Buf("U")
        U_own = sb("S_Uo", [128, G, 512], BF16)
        B_Uo = Buf("Uo")
        ER = [sb("S_Er%d" % k, [128, 2, 16, 40], F32) for k in range(2)]
        EI = [sb("S_Ei%d" % k, [128, 2, 16, 40], F32) for k in range(2)]
        B_E = Buf("E")
        AKr = sb("S_AKr", [128, 8, 32], F32)
        AKi = sb("S_AKi", [128, 8, 32], F32)
        NAKi = sb("S_NAKi", [128, 8, 32], F32)
        B_AK = Buf("AK")
        bbr = sb("S_bbr", [128, 32, 16], F32)
        bbi = sb("S_bbi", [128, 32, 16], F32)
        B_bb = Buf("bb")
        Ctr = sb("S_Ctr", [128, 32, 16], F32)
        Cti = sb("S_Cti", [128, 32, 16], F32)
        B_Ct = Buf("Ct")
        Dcol = sb("S_Dcol", [128, 32], F32)
        B_D = Buf("D")
        maskf = sb("S_mf", [128, 128], F32)
        maskb = sb("S_mb", [128, 128], F32)
        identf = sb("S_idf", [128, 128], F32)
        B_mk = Buf("mk")
        P_.dma("sp", maskf[:], C.cins["s5_maskf"].ap(), writes=[B_mk])
        P_.dma("sp", maskb[:], C.cins["s5_maskb"].ap(), writes=[B_mk])
        P_.dma("sp", identf[:], C.cins["ident_f32"].ap(), writes=[B_mk])

        st0, sb0, ps0 = _scope(C)
        with st0:
            st0.close()
            st0b, sb0, ps0 = _scope(C)
            Bp = Buf("par")
            raw = sb0("S_raw", [32, 3, 128], F32)
            ldt = sb0("S_ldt", [32, 2], F32)
            ones32 = sb0("S_ones", [32, 64], F32)
            P_.op("pool", lambda e: e.memset(ones32[:], 1.0), writes=[Bp])
            P_.dma("sp", raw[:, 0, :], ins["lam_re"][layer].rearrange("d (gp a) p -> (d gp) (a p)", a=2), writes=[Bp])
            P_.dma("sp", raw[:, 1, :], ins["lam_im"][layer].rearrange("d (gp a) p -> (d gp) (a p)", a=2), writes=[Bp])
            P_.dma("sp", ldt[:], ins["log_dt"][layer].rearrange("d (gp a) -> (d gp) a", a=2), writes=[Bp])
            for a in range(2):
                ts("dve", raw[:, 2, a * 64:(a + 1) * 64], ones32[:], ldt[:, a:a + 1], None, ALU.mult, None, [Bp], [Bp])
            pp = ps0("S_pp", [128, 3, 32], F32)
            B_pp = Buf()
            for i in range(3):
                P_.op("pe", lambda e, i=i: e.transpose(out=pp[:, i, :], in_=raw[:, i, :], identity=identf[0:32, 0:32]),
                      reads=[Bp, B_mk], writes=[B_pp])
            lam = sb0("S_lam", [128, 3, 32], F32)
            P_.op("dve", lambda e: e.tensor_copy(out=lam[:], in_=pp[:]), reads=[B_pp], writes=[Bp])
            dtv = sb0("S_dtv", [128, 32], F32)
            actf(dtv[:], lam[:, 2, :], AF.Exp, [Bp], [Bp])
            alpha = sb0("S_alpha", [128, 32], F32)
            beta = sb0("S_beta", [128, 32], F32)
            tt("dve", alpha[:], lam[:, 0, :], dtv[:], ALU.mult, [Bp], [Bp])
            tt("dve", beta[:], lam[:, 1, :], dtv[:], ALU.mult, [Bp], [Bp])
            EV = sb0("S_EV", [128, 2, 2, 40], F32)
            P_.dma("sp", EV[:], C.cins["s5_ev"].ap().rearrange("p (k d n) -> p k d n", k=2, d=2), writes=[Bp])
            halfpi = sb0("S_hpi", [128, 1], F32)
            P_.op("pool", lambda e: e.memset(halfpi[:], math.pi / 2), writes=[Bp])
            arga = sb0("S_arga", [128, 2, 16, 40], F32)
            argb = sb0("S_argb", [128, 2, 16, 40], F32)
            kk = sb0("S_kk", [128, 2, 16, 40], F32)
            sn = sb0("S_sn", [128, 2, 16, 40], F32)
            shp = [128, 2, 16, 40]
            al4 = alpha[:].rearrange("q (d g) -> q d g", d=2).unsqueeze(3).broadcast_to(shp)
            be4 = beta[:].rearrange("q (d g) -> q d g", d=2).unsqueeze(3).broadcast_to(shp)
            for kind in range(2):
                ev4 = EV[:, kind].unsqueeze(2).broadcast_to(shp)
                tt("dve", arga[:], al4, ev4, ALU.mult, [Bp], [Bp])
                actf(arga[:], arga[:], AF.Exp, [Bp], [Bp])
                tt("dve", argb[:], be4, ev4, ALU.mult, [Bp], [Bp])
                ts("dve", kk[:], argb[:], 1.0 / TWO_PI, MAGIC, ALU.mult, ALU.add, [Bp], [Bp])
                ts("dve", kk[:], kk[:], -MAGIC, None, ALU.add, None, [Bp], [Bp])
                stt(argb[:], kk[:], -CW1, argb[:], ALU.mult, ALU.add, [Bp], [Bp])
                stt(argb[:], kk[:], -CW2, argb[:], ALU.mult, ALU.add, [Bp], [Bp])
                stt(argb[:], kk[:], -CW3, argb[:], ALU.mult, ALU.add, [Bp], [Bp])
                actf(sn[:], argb[:], AF.Sin, [Bp], [Bp])
                stt(kk[:], argb[:], -1.0, argb[:], ALU.mult, ALU.min, [Bp], [Bp])
                actf(kk[:], kk[:], AF.Sin, [Bp], [Bp], bias=halfpi[:], scale=1.0)
                tt("dve", ER[kind][:], arga[:], kk[:], ALU.mult, [Bp], [B_E])
                tt("dve", EI[kind][:], arga[:], sn[:], ALU.mult, [Bp], [B_E])
            ar = sb0("S_ar", [128, 32], F32)
            ai = sb0("S_ai", [128, 32], F32)
            for (dst, src) in ((ar, ER[1]), (ai, EI[1])):
                P_.op("dve", lambda e, dst=dst, src=src: e.tensor_copy(out=dst[:, 0:16], in_=src[:, 0, :, 1]), reads=[B_E], writes=[Bp])
                P_.op("dve", lambda e, dst=dst, src=src: e.tensor_copy(out=dst[:, 16:32], in_=src[:, 1, :, 31]), reads=[B_E], writes=[Bp])
            for (dst, src) in ((AKr, ER[1]), (AKi, EI[1])):
                P_.op("dve", lambda e, dst=dst, src=src: e.tensor_copy(out=dst[:, 0, 0:16], in_=src[:, 0, :, 32]), reads=[B_E], writes=[B_AK])
                P_.op("dve", lambda e, dst=dst, src=src: e.tensor_copy(out=dst[:, 0, 16:32], in_=src[:, 1, :, 0]), reads=[B_E], writes=[B_AK])
            t32a = sb0("S_t32a", [128, 32], F32)
            t32b = sb0("S_t32b", [128, 32], F32)
            for k in range(7):
                tt("dve", t32a[:], AKr[:, k, :], AKr[:, k, :], ALU.mult, [B_AK], [Bp])
                tt("dve", t32b[:], AKi[:, k, :], AKi[:, k, :], ALU.mult, [B_AK], [Bp])
                tt("dve", AKr[:, k + 1, :], t32a[:], t32b[:], ALU.subtract, [Bp], [B_AK])
                tt("dve", t32a[:], AKr[:, k, :], AKi[:, k, :], ALU.mult, [B_AK], [Bp])
                ts("dve", AKi[:, k + 1, :], t32a[:], 2.0, None, ALU.mult, None, [Bp], [B_AK])
            ts("dve", NAKi[:], AKi[:], -1.0, None, ALU.mult, None, [B_AK], [B_AK])
            wr = sb0("S_wr", [128, 32], F32)
            wi = sb0("S_wi", [128, 32], F32)
            den = sb0("S_den", [128, 32], F32)
            ts("dve", ar[:], ar[:], -1.0, None, ALU.add, None, [Bp], [Bp])
            tt("dve", den[:], lam[:, 0, :], lam[:, 0, :], ALU.mult, [Bp], [Bp])
            tt("dve", t32a[:], lam[:, 1, :], lam[:, 1, :], ALU.mult, [Bp], [Bp])
            tt("dve", den[:], den[:], t32a[:], ALU.add, [Bp], [Bp])
            P_.op("dve", lambda e: e.reciprocal(out=den[:], in_=den[:]), reads=[Bp], writes=[Bp])
            tt("dve", t32a[:], ar[:], lam[:, 0, :], ALU.mult, [Bp], [Bp])
            tt("dve", t32b[:], ai[:], lam[:, 1, :], ALU.mult, [Bp], [Bp])
            tt("dve", wr[:], t32a[:], t32b[:], ALU.add, [Bp], [Bp])
            tt("dve", wr[:], wr[:], den[:], ALU.mult, [Bp], [Bp])
            tt("dve", t32a[:], ai[:], lam[:, 0, :], ALU.mult, [Bp], [Bp])
            tt("dve", t32b[:], ar[:], lam[:, 1, :], ALU.mult, [Bp], [Bp])
            tt("dve", wi[:], t32a[:], t32b[:], ALU.subtract, [Bp], [Bp])
            tt("dve", wi[:], wi[:], den[:], ALU.mult, [Bp], [Bp])
            Btr = sb0("S_Btr", [128, 32, 16], F32)
            Bti = sb0("S_Bti", [128, 32, 16], F32)
            P_.dma("sp", Btr[:], ins["b_re"][layer].rearrange("d (gp a) p h -> (a p) (d gp) h", a=2), writes=[Bp])
            P_.dma("sp", Bti[:], ins["b_im"][layer].rearrange("d (gp a) p h -> (a p) (d gp) h", a=2), writes=[Bp])
            tb1 = sb0("S_tb1", [128, 32, 16], F32)
            tb2 = sb0("S_tb2", [128, 32, 16], F32)
            wr3 = wr[:].unsqueeze(2).broadcast_to([128, 32, 16])
            wi3 = wi[:].unsqueeze(2).broadcast_to([128, 32, 16])
            tt("dve", tb1[:], Btr[:], wr3, ALU.mult, [Bp], [Bp])
            tt("dve", tb2[:], Bti[:], wi3, ALU.mult, [Bp], [Bp])
            tt("dve", bbr[:], tb1[:], tb2[:], ALU.subtract, [Bp], [B_bb])
            tt("dve", tb1[:], Bti[:], wr3, ALU.mult, [Bp], [Bp])
            tt("dve", tb2[:], Btr[:], wi3, ALU.mult, [Bp], [Bp])
            tt("dve", bbi[:], tb1[:], tb2[:], ALU.add, [Bp], [B_bb])
            P_.barrier(skip_cc=True)
            st0b.close()
            st0c, sb0, ps0 = _scope(C)
            Craw = [sb0("S_Craw%d" % i, [16, 2 * G * P], F32) for i in range(2)]
            P_.dma("sp", Craw[0][:].rearrange("h (d g p) -> h d g p", d=2, g=G), ins["c_re"][layer].rearrange("d g h p -> h d g p"), writes=[Bp])
            P_.dma("sp", Craw[1][:].rearrange("h (d g p) -> h d g p", d=2, g=G), ins["c_im"][layer].rearrange("d g h p -> h d g p"), writes=[Bp])
            pc = [ps0("S_pc%d" % i, [128, 32, 16], F32) for i in range(2)]
            B_pc = [Buf() for _ in range(2)]
            for i in range(2):
                for dg in range(32):
                    P_.op("pe", lambda e, i=i, dg=dg: e.transpose(out=pc[i][:, dg, :], in_=Craw[i][:, dg * 128:(dg + 1) * 128],
                                                                  identity=identf[0:16, 0:16]), reads=[Bp, B_mk], writes=[B_pc[i]])
            P_.op("dve", lambda e: e.tensor_copy(out=Ctr[:], in_=pc[0][:]), reads=[B_pc[0]], writes=[B_Ct])
            P_.op("dve", lambda e: e.tensor_copy(out=Cti[:], in_=pc[1][:]), reads=[B_pc[1]], writes=[B_Ct])
            Dsm = sb0("S_Dsm", [32, 16], F32)
            Drep = sb0("S_Drep", [32, 8, 16], F32)
            P_.dma("sp", Dsm[:], ins["d_skip"][layer], writes=[Bp])
            P_.op("dve", lambda e: e.tensor_copy(out=Drep[:], in_=Dsm[:].unsqueeze(1).broadcast_to([32, 8, 16])), reads=[Bp], writes=[Bp])
            pd = ps0("S_pd", [128, 32], F32)
            B_pd = Buf()
            P_.op("pe", lambda e: e.transpose(out=pd[:], in_=Drep[:].rearrange("g t h -> g (t h)"), identity=identf[0:32, 0:32]),
                  reads=[Bp, B_mk], writes=[B_pd])
            P_.op("dve", lambda e: e.tensor_copy(out=Dcol[:], in_=pd[:]), reads=[B_pd], writes=[B_D])
            P_.barrier(skip_cc=True)
            st0c.close()
            st0d, sb0, ps0 = _scope(C)
            Zt = [sb0("S_Zt%d" % i, [128, 8, 512], BF16) for i in range(2)]
            B_Zt = [Buf() for _ in range(2)]
            Zt2 = [sb0("S_Zt2%d" % i, [128, G * 128], BF16) for i in range(2)]
            B_Zt2 = [Buf() for _ in range(2)]
            pT = [ps0("S_pT%d" % i, [128, 8, 128], BF16) for i in range(2)]
            B_pT = [Buf() for _ in range(2)]
            npt = 0
            for (Zsrc, Bsrc, Udst, Bdst, nblk) in ((C.Zown, C.B_Zown, U_own, B_Uo, 4), (C.Zs, C.B_Zg, U_all, B_U, 8)):
                Zv = Zsrc.ap().rearrange("(j t) c -> j t c", t=8)
                for bj in range(nblk):
                    b = npt % 2
                    P_.dma("sp", Zt[b][:], Zv[bj * 128:(bj + 1) * 128, :, 512:1024], reads=[Bsrc[bj // 2]] if isinstance(Bsrc, list) else [Bsrc], writes=[B_Zt[b]])
                    P_.op("act", lambda e, b=b: e.copy(
                        out=Zt2[b][:].rearrange("j (g t h) -> j g t h", g=G, t=8),
                        in_=Zt[b][:].rearrange("j t (g h) -> j g t h", h=16)), reads=[B_Zt[b]], writes=[B_Zt2[b]])
                    for g0 in range(0, G, 8):
                        tb = npt % 2
                        npt += 1
                        for gi in range(8):
                            g = g0 + gi
                            P_.op("pe", lambda e, b=b, g=g, gi=gi, tb=tb: e.transpose(
                                out=pT[tb][:, gi, :], in_=Zt2[b][:, g * 128:(g + 1) * 128], identity=ident_bf[:]),
                                reads=[B_Zt2[b], B_const], writes=[B_pT[tb]])
                        dst = Udst[:, g0:g0 + 8, bj * 128:(bj + 1) * 128]
                        if (g0 // 8) % 2 == 0:
                            acopy(dst, pT[tb][:], [B_pT[tb]], [Bdst])
                        else:
                            P_.op("dve", lambda e, dst=dst, tb=tb: e.tensor_copy(out=dst, in_=pT[tb][:]), reads=[B_pT[tb]], writes=[Bdst])

            P_.barrier()
            st0d.close()
            if C.dbg is not None:
                o = 0
                for (nm, t_, n_) in (("ERB", ER[0], 1280), ("EIB", EI[0], 1280), ("ERC", ER[1], 1280), ("EIC", EI[1], 1280),
                                     ("bbr", bbr, 512), ("bbi", bbi, 512), ("Ctr", Ctr, 512), ("Cti", Cti, 512),
                                     ("Dcol", Dcol, 32), ("AKr", AKr, 256), ("AKi", AKi, 256)):
                    flat = t_[:]
                    if len(flat.shape) == 4:
                        flat = flat.rearrange("p a b c -> p (a b c)")
                    elif len(flat.shape) == 3:
                        flat = flat.rearrange("p a b -> p (a b)")
                    P_.dma("sp", C.dbg[:, o:o + n_], flat, reads=[B_E, B_bb, B_Ct, B_D, B_AK], writes=[Buf()])
                    C.dbg_map[nm] = (o, n_)
                    o += n_
                for g_ in range(4):
                    P_.dma("sp", C.dbgU[:, g_, :], U_all[:, g_ * 9, :], reads=[B_U], writes=[Buf()])
                P_.barrier()
            if C.bs_level == 0:
                return

        Yall = sb("S_Y", [128, 32, 64], BF16)
        se_own = [sb("S_seo%d" % i, [128, 2, 2, 128], BF16) for i in range(2)]
        B_seo = [Buf() for _ in range(2)]
        setmp = sb("S_setmp", [128, 4, 4, 32], F32)
        B_setmp = Buf()
        sel = sb("S_sel", [128, 2], F32)
        P_.dma("sp", sel[:], C.cins["s5_sel"].ap(), writes=[B_mk])

        B_Y = Buf("Yall")
        t1 = sb("S_t1", [128, 2, 40, 16], F32)
        t2 = sb("S_t2", [128, 2, 40, 16], F32)
        B_t = [Buf(), Buf()]
        t3 = sb("S_t3", [128, 2, 40, 16], F32)
        t4 = sb("S_t4", [128, 2, 40, 16], F32)
        B_t34 = [Buf(), Buf()]
        TAB = [[sb("S_tab%d_%d" % (s_, i), [128, 2, 640], BF16) for i in range(4)] for s_ in range(2)]
        B_TAB = [[Buf() for _ in range(4)] for _ in range(2)]
        Mpan = [[sb("S_M%d_%d" % (s_, a), [128, 7 * 128], BF16) for a in range(2)] for s_ in range(2)]
        B_M = [[Buf() for _ in range(2)] for _ in range(2)]
        Bm = [[sb("S_Bm%d_%d" % (s_, a), [128, 16, 64], BF16) for a in range(2)] for s_ in range(2)]
        B_Bm = [[Buf() for _ in range(2)] for _ in range(2)]
        SA = sb("S_SA", [128, 2, 2, JC], F32)
        SB_ = sb("S_SB", [128, 2, 2, JC], F32)
        B_SA = Buf()
        B_SB = Buf()
        Sent = [sb("S_Sent%d" % i, [128, 2, 2, JC], BF16) for i in range(2)]
        B_Sent = [Buf() for _ in range(2)]
        dtmp = sb("S_dtmp", [128, 128], F32)
        dtmp2 = sb("S_dtmp2", [128, 128], F32)
        B_dt = Buf()
        pm = ps("S_pm", [128, 8, 128], F32)
        B_pm = Buf()
        pbm = ps("S_pbm", [128, 16, 64], BF16)
        B_pbm = Buf()
        Xps = ps("S_X", [128, 2, 2, JC], F32)
        B_X = Buf()
        Yps = [ps("S_Yp%d" % i, [128, 512], F32) for i in range(2)]
        B_Yp = [Buf() for _ in range(2)]
        for i in range(2):
            P_.op("pool", lambda e, i=i: e.memset(Sent[i][:], 0.0), writes=[B_Sent[i]])
        Uv = U_all[:].rearrange("p g (j q) -> p g q j", q=4)
        bbr4 = bbr[:].rearrange("q (d g) h -> q d g h", d=2)
        bbi4 = bbi[:].rearrange("q (d g) h -> q d g h", d=2)
        Ctr4 = Ctr[:].rearrange("q (d g) h -> q d g h", d=2)
        Cti4 = Cti[:].rearrange("q (d g) h -> q d g h", d=2)
        shp = [128, 2, 40, 16]
        nyp = 0
        YSv = C.YS.ap().rearrange("(j t) c -> j t c", t=32)
        specs = (
            (0, 0, bbr4, bbi4, "sub"),
            (1, 0, bbi4, bbr4, "add"),
            (2, 1, Ctr4, Cti4, "sub"),
            (3, 1, Cti4, Ctr4, "nadd"),
        )
        slots = []
        for m in (3, 2, 1):
            slots.append((1, 0, 1, 32 - 8 * m))
        slots.append((0, 31, 0, 0))
        slots.append((1, 0, 1, 32))
        for dl in (1, 2, 3):
            slots.append((0, 24, 0, 8 * dl - 7))

        def stage_T(gp):
            s_ = gp % 2
            tab, btab = TAB[s_], B_TAB[s_]
            for (oi, kind, Y1, Y2, comb) in specs:
                e1 = ER[kind][:, :, gp, :].unsqueeze(3).broadcast_to(shp)
                p1 = Y1[:, :, gp, :].unsqueeze(2).broadcast_to(shp)
                e2 = EI[kind][:, :, gp, :].unsqueeze(3).broadcast_to(shp)
                p2 = Y2[:, :, gp, :].unsqueeze(2).broadcast_to(shp)
                if oi < 2 and POOL_TABLES:
                    ta, tb_, Bt_, peng = t3, t4, B_t34, "pool"
                else:
                    ta, tb_, Bt_, peng = t1, t2, B_t, "dve"
                tt(peng, ta[:], e1, p1, ALU.mult, [B_E, B_bb, B_Ct], [Bt_[0]])
                tt(peng, tb_[:], e2, p2, ALU.mult, [B_E, B_bb, B_Ct], [Bt_[1]])
                outv = tab[oi][:].rearrange("q d (n h) -> q d n h", h=16)
                if comb == "sub":
                    tt(peng, outv, ta[:], tb_[:], ALU.subtract, Bt_, [btab[oi]])
                elif comb == "add":
                    tt(peng, outv, ta[:], tb_[:], ALU.add, Bt_, [btab[oi]])
                else:
                    stt(outv, ta[:], -1.0, tb_[:], ALU.mult, ALU.subtract, Bt_, [btab[oi]])

        def stage_PX(gp):
            s_ = gp % 2
            tab, btab = TAB[s_], B_TAB[s_]
            for gpar in range(2):
                g = 2 * gp + gpar
                for d in range(2):
                    for q in range(4):
                        for x in range(2):
                            idx = (d * 4 + q) * 2 + x
                            P_.op("pe", lambda e, idx=idx, gpar=gpar, i_=tab[x][64 * gpar:64 * gpar + 64, d, q * 128:(q + 1) * 128]: e.transpose(
                                out=pbm[:, idx, :], in_=i_,
                                identity=ident_bf[64 * gpar:64 * gpar + 64, 64 * gpar:64 * gpar + 64]),
                                reads=[btab[0], btab[1], B_const], writes=[B_pbm])
                acopy(Bm[s_][gpar][:], pbm[:], [B_pbm], [B_Bm[s_][gpar]])
                for d in range(2):
                    for x in range(2):
                        for q in range(4):
                            idx = (d * 4 + q) * 2 + x
                            P_.op("pe", lambda e, d=d, x=x, q=q, gpar=gpar, l_=Bm[s_][gpar][:, idx, :], r_=Uv[:, g, q, :]: e.matmul(
                                Xps[64 * gpar:64 * gpar + 64, d, x, :], lhsT=l_, rhs=r_,
                                start=(q == 0), stop=(q == 3)), reads=[B_Bm[s_][gpar], B_U], writes=[B_X])

        def stage_PM(gp):
            s_ = gp % 2
            tab, btab = TAB[s_], B_TAB[s_]

            def bsl(oi, gpar, d, n0):
                return tab[oi][64 * gpar:64 * gpar + 64, d, n0 * 16:(n0 + 8) * 16]
            for gpar in range(2):
                g = 2 * gp + gpar
                for si, (db, nb, dc, ncc) in enumerate(slots):
                    for x in range(2):
                        P_.op("pe", lambda e, si=si, x=x, l_=bsl(0 + x, gpar, db, nb), r_=bsl(2 + x, gpar, dc, ncc): e.matmul(
                            pm[:, si, :], lhsT=l_, rhs=r_,
                            start=(x == 0), stop=(x == 1)), reads=[btab[0], btab[1], btab[2], btab[3]], writes=[B_pm])
                Mp3 = Mpan[s_][gpar][:].rearrange("p (s n) -> p s n", s=7)
                acopy(Mp3[:, 0:3, :], pm[:, 0:3, :], [B_pm], [B_M[s_][gpar]])
                acopy(Mp3[:, 4:7, :], pm[:, 5:8, :], [B_pm], [B_M[s_][gpar]])
                acopy(dtmp[:], pm[:, 3, :], [B_pm], [B_dt])
                acopy(dtmp2[:], pm[:, 4, :], [B_pm], [B_dt])
                tt("dve", dtmp[:], dtmp[:], maskf[:], ALU.mult, [B_dt, B_mk], [B_dt])
                tt("dve", dtmp2[:], dtmp2[:], maskb[:], ALU.mult, [B_dt, B_mk], [B_dt])
                tt("dve", dtmp[:], dtmp[:], dtmp2[:], ALU.add, [B_dt], [B_dt])
                stt(Mp3[:, 3, :], identf[:], Dcol[:, g:g + 1], dtmp[:], ALU.mult, ALU.add, [B_dt, B_D, B_mk], [B_M[s_][gpar]])

        def stage_SC(gp):
            s_ = gp % 2
            acopy(SA[:], Xps[:], [B_X], [B_SA])
            Bm_ = {(bn, d, x): Buf() for bn in range(2) for d in range(2) for x in range(2)}
            Bu_ = {(bn, d): Buf() for bn in range(2) for d in range(2)}
            for key in Bm_:
                if key[0] == 0:
                    Bm_[key].w = B_SA.w
            bufs = (SA, SB_)
            for k in range(8):
                sh = 1 << k
                si, di = k % 2, (k + 1) % 2
                src, dst = bufs[si], bufs[di]
                plan = []
                for d in range(2):
                    col = d * 16 + gp
                    if d == 0:
                        lo, shf, unt = slice(sh, JC), slice(0, JC - sh), slice(0, sh)
                    else:
                        lo, shf, unt = slice(0, JC - sh), slice(sh, JC), slice(JC - sh, JC)
                    plan.append((d, lo, shf, unt, AKr[:, k, col:col + 1], AKi[:, k, col:col + 1], NAKi[:, k, col:col + 1]))

                def rd(d, si=si):
                    return [Bm_[(si, d, 0)], Bm_[(si, d, 1)], Bu_[(si, d)], B_AK]
                for (d, lo, shf, unt, cr, ci, nci) in plan:
                    stt(dst[:, d, 0, lo], src[:, d, 0, shf], cr, src[:, d, 0, lo], ALU.mult, ALU.add, rd(d), [Bm_[(di, d, 0)]])
                    stt(dst[:, d, 1, lo], src[:, d, 1, shf], cr, src[:, d, 1, lo], ALU.mult, ALU.add, rd(d), [Bm_[(di, d, 1)]])
                for (d, lo, shf, unt, cr, ci, nci) in plan:
                    stt(dst[:, d, 0, lo], src[:, d, 1, shf], nci, dst[:, d, 0, lo], ALU.mult, ALU.add, rd(d), [Bm_[(di, d, 0)]])
                    stt(dst[:, d, 1, lo], src[:, d, 0, shf], ci, dst[:, d, 1, lo], ALU.mult, ALU.add, rd(d), [Bm_[(di, d, 1)]])
                    acopy(dst[:, d, :, unt], src[:, d, :, unt], rd(d), [Bu_[(di, d)]])
            parts = [Bm_[(0, d, x)] for d in range(2) for x in range(2)] + [Bu_[(0, d)] for d in range(2)]
            se = Sent[s_]
            acopy(se[:, 0, :, 1:JC], SA[:, 0, :, 0:JC - 1], parts, [B_Sent[s_]])
            acopy(se[:, 1, :, 0:JC - 1], SA[:, 1, :, 1:JC], parts, [B_Sent[s_]])
            B_SA.w = None
            B_SA.r = {}
            for b__ in list(Bm_.values()) + list(Bu_.values()):
                for e__ in ([b__.w] if b__.w else []) + list(b__.r.items()):
                    if B_SA.r.get(e__[0], 0) < e__[1]:
                        B_SA.r[e__[0]] = e__[1]
            sev = se[:].rearrange("q d x (c r j) -> q (d x) c r j", c=4, r=2)
            seo = se_own[s_]
            ts("dve", setmp[:], sev[:, :, :, 0, :], sel[:, 0:1], None, ALU.mult, None, [B_Sent[s_], B_mk], [B_setmp])
            stt(seo[:].rearrange("q d x (c j) -> q (d x) c j", c=4), sev[:, :, :, 1, :], sel[:, 1:2], setmp[:], ALU.mult, ALU.add,
                [B_Sent[s_], B_mk, B_setmp], [B_seo[s_]])

        nyp_ = [0]

        def stage_OUT(gp):
            s_ = gp % 2
            tab, btab = TAB[s_], B_TAB[s_]
            seo = se_own[s_]
            for gpar in range(2):
                g = 2 * gp + gpar
                Mp = Mpan[s_][gpar]
                yb = nyp_[0] % 2
                nyp_[0] += 1
                for k in range(4):
                    P_.op("pe", lambda e, k=k, yb=yb, l_=U_own[:, g, k:512:4], r_=Mp[:, (3 - k) * 128:(7 - k) * 128]: e.matmul(
                        Yps[yb][:], lhsT=l_, rhs=r_, start=(k == 0), stop=False), reads=[B_Uo, B_M[s_][gpar]], writes=[B_Yp[yb]])
                for d in range(2):
                    n0 = 1 if d == 0 else 0
                    for x in range(2):
                        P_.op("pe", lambda e, d=d, x=x, yb=yb, l_=seo[64 * gpar:64 * gpar + 64, d, x, :],
                              r_=tab[2 + x][64 * gpar:64 * gpar + 64, d, n0 * 16:n0 * 16 + 512]: e.matmul(
                            Yps[yb][:], lhsT=l_, rhs=r_,
                            start=False, stop=(d == 1 and x == 1)), reads=[B_seo[s_], btab[2], btab[3]], writes=[B_Yp[yb]])
                gl = g % 4
                acopy(Yall[:, :, gl * 16:(gl + 1) * 16], Yps[yb][:].rearrange("j (t h) -> j t h", h=16), [B_Yp[yb]], [B_Y])
            if gp % 2 == 1:
                q8 = gp // 2
                P_.dma("sp", YSv[:, :, q8 * 64:(q8 + 1) * 64], Yall[:], reads=[B_Y], writes=[C.B_YS])

        if not BS_PIPE:
            for gp in range(16):
                stage_T(gp)
                stage_PM(gp)
                stage_PX(gp)
                stage_SC(gp)
                stage_OUT(gp)
        else:
            stage_T(0)
            stage_PX(0)
            stage_PM(0)
            for gp in range(16):
                if gp + 1 < 16:
                    stage_T(gp + 1)
                stage_SC(gp)
                if gp + 1 < 16:
                    stage_PX(gp + 1)
                    stage_PM(gp + 1)
                stage_OUT(gp)
        P_.barrier()


def prefetch_C_consts(C, layer, sb, defer=False):
    P_ = C.P
    ins = C.ins
    pre = {}
    Bw = pre["Bw"] = Buf("Cw")
    wout = pre["wout"] = sb("C_wout", [128, 8, D], BF16)
    wglu = pre["wglu"] = sb("C_wglu", [128, 4, DS], BF16)
    ones_bf = pre["ones_bf"] = sb("C_ones", [128, 128], BF16)
    gvec = pre["gvec"] = {}
    for nm in ("post_mix_g", "pre_mlp_g", "post_mlp_g"):
        gvec[nm] = sb("C_" + nm, [128, D], F32)
    pv_raw = pre["pv_raw"] = sb("C_pvraw", [4, 3, 128], F32)
    identf = pre["identf"] = sb("C_idf", [128, 128], F32)

    def issue():
        P_.dma("sp", wout[:], C.wb_out[layer].rearrange("(ct p) c -> p ct c", p=128), reads=[C.B_wb["out"]], writes=[Bw])
        P_.dma("sp", wglu[:], C.wb_glu[layer].rearrange("(ct p) c -> p ct c", p=128), reads=[C.B_wb["glu"]], writes=[Bw])
        P_.dma("sp", ones_bf[:], C.cins["ones_bf"].ap(), writes=[Bw])
        for nm in ("post_mix_g", "pre_mlp_g", "post_mlp_g"):
            P_.dma("sp", gvec[nm][:], ins[nm][layer].partition_broadcast(128), writes=[Bw])
        for k_, nm in enumerate(("fourier_out_g", "ssm_out_g", "b_glu")):
            P_.dma("sp", pv_raw[:, k_, :], ins[nm][layer].rearrange("(ct p) -> ct p", p=128), writes=[Bw])
        P_.dma("sp", identf[:], C.cins["ident_f32"].ap(), writes=[Bw])
    if defer:
        pre["issue"] = issue
    else:
        issue()
    return pre


def phase_C(C, layer, x_src, B_xsrc, x_dst, B_xdst, pre=None):
    P_ = C.P
    ins = C.ins
    ident_bf = C.ident_bf
    B_const = C.B_const
    st, sb, ps = _scope(C)

    def mm(out, lhsT, rhs, start, stop, R, W):
        P_.op("pe", lambda e: e.matmul(out, lhsT=lhsT, rhs=rhs, start=start, stop=stop), reads=R, writes=W)

    def tr(out, in_, ident, R, W):
        P_.op("pe", lambda e: e.transpose(out=out, in_=in_, identity=ident), reads=R, writes=W)

    def tt(out, in0, in1, op, R, W):
        P_.op("dve", lambda e: e.tensor_tensor(out=out, in0=in0, in1=in1, op=op), reads=R, writes=W)

    def stt(out, in0, scalar, in1, op0, op1, R, W):
        P_.op("dve", lambda e: e.scalar_tensor_tensor(out=out, in0=in0, scalar=scalar, in1=in1, op0=op0, op1=op1), reads=R, writes=W)

    def actf(out, in_, func, R, W, **kw):
        P_.op("act", lambda e: e.activation(out=out, in_=in_, func=func, **kw), reads=R, writes=W)

    def dcopy(out, in_, R, W):
        P_.op("dve", lambda e: e.tensor_copy(out=out, in_=in_), reads=R, writes=W)

    def recip(out, in_, R, W):
        P_.op("dve", lambda e: e.reciprocal(out=out, in_=in_), reads=R, writes=W)

    with st:
        if pre is None:
            pre = prefetch_C_consts(C, layer, sb)
        Bw, wout, wglu, ones_bf, gvec, pv_raw, identf = (pre[k] for k in ("Bw", "wout", "wglu", "ones_bf", "gvec", "pv_raw", "identf"))
        pv = sb("C_pv", [128, 3, 4], F32)
        epsb = sb("C_eps", [128, 1], F32)
        P_.op("pool", lambda e: e.memset(epsb[:], EPS), writes=[Bw])
        bank = [ps("C_bank%d" % i, [128, 512], F32) for i in range(8)]
        bankbf = [b_.bitcast(BF16) for b_ in bank]
        B_bank = [Buf("bank%d" % i) for i in range(8)]
        nb = [0]

        def getbank():
            i = nb[0] % 8
            nb[0] += 1
            return i
        bi = getbank()
        for k_ in range(3):
            tr(bank[bi][:, k_ * 4:(k_ + 1) * 4], pv_raw[:, k_, :], identf[0:4, 0:4], [Bw], [B_bank[bi]])
        dcopy(pv[:].rearrange("p a b -> p (a b)"), bank[bi][:, 0:12], [B_bank[bi]], [Bw])
        gf = pv[:, 0, :]
        gs = pv[:, 1, :]
        bgl = pv[:, 2, :]
        wbuf = [sb("C_wb%d" % i, [128, 8, 1024], BF16) for i in range(3)]
        B_wbuf = [Buf() for _ in range(3)]
        nw = [0]
        xt = sb("C_xt", [128, 4, D], F32)
        B_xt = Buf()
        yfT = sb("C_yfT", [128, 4, TT], F32)
        B_yfT = Buf()
        ysT = sb("C_ysT", [128, 4, DS], BF16)
        B_ysT = Buf()
        sq = sb("C_sq", [128, 4, TT], BF16)
        B_sq = Buf()
        rf = sb("C_rf", [128, TT], F32)
        B_rf = Buf()
        catT = sb("C_catT", [128, 8, TT], BF16)
        B_cat = Buf()
        gT = sb("C_gT", [128, 4, TT], BF16)
        B_gT = Buf()
        oT = sb("C_oT", [128, 4, TT], F32)
        B_oT = Buf()
        sg = sb("C_sg", [128, TT], BF16)
        B_sg = Buf()
        SC = [sb("C_SC%d" % i, [128, D], F32) for i in range(2)]
        B_SC = [Buf() for _ in range(2)]
        ssq = sb("C_ssq", [128, 8], F32)
        B_ssq = Buf()
        rs4 = sb("C_rs4", [128, 4], F32)
        B_rs4 = Buf()
        junk = sb("C_junk", [128, D], BF16)
        B_junk = Buf()
        h2 = sb("C_h2", [128, 4, D], BF16)
        B_h2 = Buf()
        h2T = sb("C_h2T", [128, 8, TT], BF16)
        B_h2T = Buf()
        aT = sb("C_aT", [128, 32, TT], BF16)
        B_aT = Buf()
        rl = [sb("C_rl%d" % i, [128, TT], F32) for i in range(2)]
        B_rl = [Buf() for _ in range(2)]
        nrl = [0]
        w1v = C.wb_ff1[layer].rearrange("(kt p) h -> p kt h", p=128)
        w2v = C.wb_ff2[layer].rearrange("(ht p) c -> p ht c", p=128)

        def wload(kind, c):
            i = nw[0] % 3
            nw[0] += 1
            if NOWLOAD and nw[0] > 3:
                return i
            if kind == 1:
                P_.dma("sp", wbuf[i][:], w1v[:, :, c * 1024:(c + 1) * 1024], reads=[C.B_wb["ff1"]], writes=[B_wbuf[i]])
            else:
                P_.dma("sp", wbuf[i][:], w2v[:, c * 8:(c + 1) * 8, :], reads=[C.B_wb["ff2"]], writes=[B_wbuf[i]])
            return i

        def rstd_from_sum(dst, src, n, R, W):
            actf(dst, src, AF.Sqrt, R, W, bias=epsb[:], scale=1.0 / n)
            recip(dst, dst, W, W)

        def residual_norm(pbanks, gname):
            g_ = gvec[gname]
            for s in range(4):
                for hf in range(2):
                    b_ = pbanks[s][hf]
                    actf(junk[:, 0:512], bank[b_][:], AF.Square, [B_bank[b_]], [B_junk, B_ssq], accum_out=ssq[:, s * 2 + hf:s * 2 + hf + 1])
            tt(rs4[:], ssq[:].rearrange("p (s h) -> p s h", h=2)[:, :, 0], ssq[:].rearrange("p (s h) -> p s h", h=2)[:, :, 1], ALU.add, [B_ssq], [B_rs4])
            P_.op("dve", lambda e: e.tensor_scalar(out=rs4[:], in0=rs4[:], scalar1=1.0 / D, scalar2=EPS, op0=ALU.mult, op1=ALU.add),
                  reads=[B_rs4], writes=[B_rs4])
            actf(rs4[:], rs4[:], AF.Sqrt, [B_rs4], [B_rs4])
            recip(rs4[:], rs4[:], [B_rs4], [B_rs4])
            for s in range(4):
                sc = s % 2
                for hf in range(2):
                    b_ = pbanks[s][hf]
                    actf(SC[sc][:, hf * 512:(hf + 1) * 512], bank[b_][:], AF.Copy, [B_bank[b_], B_rs4], [B_SC[sc]], scale=rs4[:, s:s + 1])
                tt(SC[sc][:], SC[sc][:], g_[:], ALU.mult, [B_SC[sc], Bw], [B_SC[sc]])
                tt(xt[:, s, :], xt[:, s, :], SC[sc][:], ALU.add, [B_SC[sc], B_xt], [B_xt])

        def tail_thunks(i):
            t0 = i * TT
            th = []
            st_ = {}

            def a1():
                P_.dma("sp", yfT[:], C.YF.ap().rearrange("(ct p) l -> p ct l", p=128)[:, :, t0:t0 + TT], reads=[C.B_YF], writes=[B_yfT])
                P_.dma("sp", ysT[:], C.YS[t0:t0 + TT, :].rearrange("(s p) c -> p s c", p=128), reads=[C.B_YS], writes=[B_ysT])
            th.append(a1)
            th.append(lambda: actf(sq[:], yfT[:], AF.Square, [B_yfT], [B_sq]))

            def a3():
                bi = getbank()
                st_["b1"] = bi
                for ct in range(4):
                    mm(bank[bi][:], ones_bf[:], sq[:, ct, :], ct == 0, ct == 3, [B_sq, Bw], [B_bank[bi]])
            th.append(a3)
            th.append(lambda: rstd_from_sum(rf[:], bank[st_["b1"]][:], DF, [B_bank[st_["b1"]], Bw], [B_rf]))

            def a5():
                for ct in range(4):
                    stt(catT[:, ct, :], yfT[:, ct, :], gf[:, ct:ct + 1], rf[:], ALU.mult, ALU.mult, [B_yfT, B_rf, Bw], [B_cat])
            th.append(a5)
            for s in range(4):
                def tr_s(s=s):
                    bi = getbank()
                    st_["t%d" % s] = bi
                    for ct in range(4):
                        tr(bankbf[bi][:, ct * 128:(ct + 1) * 128], ysT[:, s, ct * 128:(ct + 1) * 128], ident_bf[:], [B_ysT, B_const], [B_bank[bi]])
                th.append(tr_s)

                def ge_s(s=s):
                    bi = st_["t%d" % s]
                    actf(gT[:, :, s * 128:(s + 1) * 128], bankbf[bi][:, 0:512].rearrange("p (c t) -> p c t", c=4), AF.Gelu_apprx_tanh,
                         [B_bank[bi]], [B_gT])
                th.append(ge_s)
            for co in range(4):
                def glu_mm(co=co):
                    bi = getbank()
                    st_["g%d" % co] = bi
                    for ci in range(4):
                        mm(bank[bi][:], wglu[:, ci, co * 128:(co + 1) * 128], gT[:, ci, :], ci == 0, ci == 3, [Bw, B_gT], [B_bank[bi]])
                th.append(glu_mm)

                def glu_ev(co=co):
                    bi = st_["g%d" % co]
                    actf(sg[:], bank[bi][:], AF.Sigmoid, [B_bank[bi], Bw], [B_sg], bias=bgl[:, co:co + 1], scale=1.0)
                    tt(oT[:, co, :], gT[:, co, :], sg[:], ALU.mult, [B_gT, B_sg], [B_oT])
                th.append(glu_ev)
            th.append(lambda: actf(sq[:], oT[:], AF.Square, [B_oT], [B_sq]))

            def b3():
                bi = getbank()
                st_["b2"] = bi
                for ct in range(4):
                    mm(bank[bi][:], ones_bf[:], sq[:, ct, :], ct == 0, ct == 3, [B_sq, Bw], [B_bank[bi]])
            th.append(b3)
            th.append(lambda: rstd_from_sum(rf[:], bank[st_["b2"]][:], DS, [B_bank[st_["b2"]], Bw], [B_rf]))

            def b5():
                for ct in range(4):
                    stt(catT[:, 4 + ct, :], oT[:, ct, :], gs[:, ct:ct + 1], rf[:], ALU.mult, ALU.mult, [B_oT, B_rf, Bw], [B_cat])
            th.append(b5)
            return th

        wnext = wload(1, 0)
        for f_ in tail_thunks(0):
            f_()
        for i in range(NT):
            t0 = i * TT
            pending = tail_thunks(i + 1) if i + 1 < NT else []
            P_.dma("sp", xt[:], x_src[t0:t0 + TT, :].rearrange("(s p) d -> p s d", p=128), reads=[B_xsrc], writes=[B_xt])
            pb = [[None, None] for _ in range(4)]
            for s in range(4):
                for hf in range(2):
                    bi = getbank()
                    pb[s][hf] = bi
                    for ct in range(8):
                        mm(bank[bi][:], catT[:, ct, s * 128:(s + 1) * 128], wout[:, ct, hf * 512:(hf + 1) * 512], ct == 0, ct == 7,
                           [B_cat, Bw], [B_bank[bi]])
            residual_norm(pb, "post_mix_g")
            for s in range(4):
                actf(junk[:], xt[:, s, :], AF.Square, [B_xt], [B_junk, B_ssq], accum_out=ssq[:, s:s + 1])
            P_.op("dve", lambda e: e.tensor_scalar(out=rs4[:], in0=ssq[:, 0:4], scalar1=1.0 / D, scalar2=EPS, op0=ALU.mult, op1=ALU.add),
                  reads=[B_ssq], writes=[B_rs4])
            actf(rs4[:], rs4[:], AF.Sqrt, [B_rs4], [B_rs4])
            recip(rs4[:], rs4[:], [B_rs4], [B_rs4])
            for s in range(4):
                stt(h2[:, s, :], xt[:, s, :], rs4[:, s:s + 1], gvec["pre_mlp_g"][:], ALU.mult, ALU.mult, [B_xt, B_rs4, Bw], [B_h2])
            for s in range(4):
                bi = getbank()
                for kt in range(8):
                    tr(bankbf[bi][:, kt * 128:(kt + 1) * 128], h2[:, s, kt * 128:(kt + 1) * 128], ident_bf[:], [B_h2, B_const], [B_bank[bi]])
                src = bankbf[bi][:].rearrange("p (k t) -> p k t", k=8)
                if s % 2 == 0:
                    P_.op("act", lambda e, s=s, src=src: e.copy(out=h2T[:, :, s * 128:(s + 1) * 128], in_=src), reads=[B_bank[bi]], writes=[B_h2T])
                else:
                    dcopy(h2T[:, :, s * 128:(s + 1) * 128], src, [B_bank[bi]], [B_h2T])
            for c in range(4):
                wi_ = wnext
                wnext = wload(1, c + 1) if c < 3 else wload(2, 0)
                for hl in range(8):
                    ht = c * 8 + hl
                    bi = getbank()
                    for kt in range(8):
                        mm(bank[bi][:], wbuf[wi_][:, kt, hl * 128:(hl + 1) * 128], h2T[:, kt, :], kt == 0, kt == 7, [B_wbuf[wi_], B_h2T], [B_bank[bi]])
                    r_ = nrl[0] % 2
                    nrl[0] += 1
                    actf(rl[r_][:], bank[bi][:], AF.Relu, [B_bank[bi]], [B_rl[r_]])
                    if ht % 2 == 0:
                        tt(aT[:, ht, :], rl[r_][:], rl[r_][:], ALU.mult, [B_rl[r_]], [B_aT])
                    else:
                        actf(aT[:, ht, :], rl[r_][:], AF.Square, [B_rl[r_]], [B_aT])
                    if pending and ht >= 8:
                        pending.pop(0)()
            while pending:
                pending.pop(0)()
            pb = [[s * 2 + hf for hf in range(2)] for s in range(4)]
            nb[0] = 0
            for c in range(4):
                wi_ = wnext
                if c < 3:
                    wnext = wload(2, c + 1)
                elif i + 1 < NT:
                    wnext = wload(1, 0)
                for s in range(4):
                    for hf in range(2):
                        bi = pb[s][hf]
                        for hl in range(8):
                            ht = c * 8 + hl
                            mm(bank[bi][:], aT[:, ht, s * 128:(s + 1) * 128], wbuf[wi_][:, hl, hf * 512:(hf + 1) * 512],
                               (c == 0 and hl == 0), (c == 3 and hl == 7), [B_aT, B_wbuf[wi_]], [B_bank[bi]])
            residual_norm(pb, "post_mlp_g")
            P_.dma("pool", x_dst[t0:t0 + TT, :].rearrange("(s p) d -> p s d", p=128), xt[:], reads=[B_xt], writes=[B_xdst])
        P_.barrier()


def build(debug=None, stop_after=None, bs_level=9, skipA=False):
    debug = debug or set()
    nc = bass.Bass("TRN2", target_bir_lowering=False)
    C = Ctx()
    C.nc = nc
    C.P = P_ = Prog(nc)
    dt = nc.dram_tensor

    def kind(nm):
        return "ExternalOutput" if nm in debug else "Internal"

    x_in = dt("x", [LO, D], F32, kind="ExternalInput")
    C.ins = {}
    shapes = {
        "w_in": [DEPTH, D, D], "w_out": [DEPTH, D, D],
        "pre_mix_g": [DEPTH, D], "post_mix_g": [DEPTH, D], "pre_mlp_g": [DEPTH, D], "post_mlp_g": [DEPTH, D],
        "fourier_out_g": [DEPTH, DF], "ssm_out_g": [DEPTH, DS],
        "w_fourier": [DEPTH, NH, 64, 64],
        "lam_re": [DEPTH, 2, G, P], "lam_im": [DEPTH, 2, G, P], "log_dt": [DEPTH, 2, G],
        "b_re": [DEPTH, 2, G, P, H], "b_im": [DEPTH, 2, G, P, H],
        "c_re": [DEPTH, 2, G, H, P], "c_im": [DEPTH, 2, G, H, P],
        "d_skip": [DEPTH, G, H], "w_glu": [DEPTH, DS, DS], "b_glu": [DEPTH, DS],
        "w_ff1": [DEPTH, D, DFF], "w_ff2": [DEPTH, DFF, D],
    }
    for k, s in shapes.items():
        C.ins[k] = dt(k, s, F32, kind="ExternalInput")
    consts = _consts(0)
    C.cins = {}
    for k, v in consts.items():
        C.cins[k] = dt(k, list(v.shape), BF16 if v.dtype == ml_dtypes.bfloat16 else F32, kind="ExternalInput")
    y_out = dt("y", [LO, D], F32, kind="ExternalOutput")

    C.wb_in = dt("wb_in", [DEPTH, D, D], BF16)
    C.wb_out = dt("wb_out", [DEPTH, D, D], BF16)
    C.wb_glu = dt("wb_glu", [DEPTH, DS, DS], BF16)
    C.wb_ff1 = dt("wb_ff1", [DEPTH, D, DFF], BF16)
    C.wb_ff2 = dt("wb_ff2", [DEPTH, DFF, D], BF16)
    C.Zown = dt("zown", [LO, D], BF16)
    C.Zs = dt("zs", [L, D], BF16, addr_space="Local")
    C.YF = dt("yf", [DF, LO], F32, kind=kind("yf"))
    C.YS = dt("ysm", [LO, DS], BF16, kind=kind("ysm"))
    C.X1 = dt("x1", [LO, D], F32, kind=kind("x1"))
    C.B_Zown = Buf("Zown")
    C.B_Zg = [Buf("Zg%d" % i) for i in range(4)]
    C.B_wb = {k: Buf(k) for k in ("in", "out", "glu", "ff1", "ff2")}
    C.B_Z = Buf("Z")
    C.B_YF = Buf("YF")
    C.B_YS = Buf("YS")
    C.dbg = dt("dbg", [128, 8192], F32, kind="ExternalOutput") if "dbg" in debug else None
    C.dbgU = dt("dbgU", [128, 4, 1024], BF16, kind="ExternalOutput") if "dbg" in debug else None
    C.dbgB = dt("dbgB", [128, 8192], BF16, kind="ExternalOutput") if "dbg" in debug else None
    C.dbgF = dt("dbgF", [128, 4096], F32, kind="ExternalOutput") if "dbg" in debug else None
    C.dbg_map = {}
    C.bs_level = bs_level

    st, sb, ps = _scope(C)
    with st:
        def cast_weights(which):
            for (i, key) in which:
                if key in ("in", "out", "glu"):
                    src, dst = {"in": (C.ins["w_in"], C.wb_in), "out": (C.ins["w_out"], C.wb_out), "glu": (C.ins["w_glu"], C.wb_glu)}[key]
                    P_.dma("pool", dst[i], src[i], writes=[C.B_wb[key]])
                elif key == "ff1":
                    for q in range(4):
                        P_.dma("pool", C.wb_ff1[i, q * 256:(q + 1) * 256, :].rearrange("r (a c) -> (r a) c", c=1024),
                               C.ins["w_ff1"][i, q * 256:(q + 1) * 256, :].rearrange("r (a c) -> (r a) c", c=1024),
                               writes=[C.B_wb["ff1"]])
                else:
                    for q in range(4):
                        P_.dma("pool", C.wb_ff2[i, q * 1024:(q + 1) * 1024, :], C.ins["w_ff2"][i, q * 1024:(q + 1) * 1024, :],
                               writes=[C.B_wb["ff2"]])
        C.ident_bf = sb("ident_bf", [128, 128], BF16)
        C.B_const = Buf("const")
        P_.dma("sp", C.ident_bf[:], C.cins["ident_bf"].ap(), writes=[C.B_const])

        B_x = Buf("x")
        B_x1 = Buf("x1")
        B_y = Buf("y")
        srcs = [(x_in, B_x, C.X1, B_x1), (C.X1, B_x1, y_out, B_y)]
        for layer in range(DEPTH if stop_after is None else 1):
            xs, Bxs, xd, Bxd = srcs[layer]
            if not skipA:
                phase_A(C, layer, xs, Bxs)
            if layer == 0:
                cast_weights([(0, "out"), (0, "glu"), (0, "ff1"), (0, "ff2")])
            if stop_after in ("BS", "C", None):
                phase_BS(C, layer)
            if layer == 0:
                cast_weights([(1, "out"), (1, "glu"), (1, "ff1"), (1, "ff2")])
            stc, sbc, psc = _scope(C)
            with stc:
                pre = prefetch_C_consts(C, layer, sbc, defer=True) if stop_after in ("C", None) else None
                if stop_after in ("BF", "C", None) and not skipA:
                    phase_BF(C, layer, after_loads=pre["issue"] if pre else None)
                elif pre:
                    pre["issue"]()
                if stop_after in ("C", None):
                    phase_C(C, layer, xs, Bxs, xd, Bxd, pre=pre)
        P_.barrier()
        P_.emit()
    C_last[0] = C
    return nc, consts


C_last = [None]


def _run(inputs, debug=None, n_cores=8, stop_after=None, bs_level=9, trace=False):
    nc, _ = build(debug, stop_after, bs_level)
    in_maps = []
    x = np.ascontiguousarray(inputs["x"], dtype=np.float32)
    shared = {k: np.ascontiguousarray(v, dtype=np.float32) for k, v in inputs.items() if k != "x"}
    cst = [_consts(0), _consts(1)]
    for c in range(n_cores):
        b, r = c // 2, c % 2
        m = {"x": np.ascontiguousarray(x[b].reshape(4, 2, 1024, D)[:, r].reshape(LO, D))}
        m.update(shared)
        m.update(cst[r])
        in_maps.append(m)
    if trace:
        return run_bass_kernel_spmd(nc, in_maps, core_ids=list(range(n_cores)), trace=True)
    return run_bass_kernel_spmd(nc, in_maps, core_ids=list(range(n_cores)))


def kernel(**inputs):
    res = _run(inputs, n_cores=8)
    out = np.empty((4, L, D), dtype=np.float32)
    for c in range(8):
        b, r = c // 2, c % 2
        out[b].reshape(4, 2, 1024, D)[:, r] = np.asarray(res.results[c]["y"], dtype=np.float32).reshape(4, 1024, D)
    return out
```
